# Optimizing a Trainium2 kernel written in Bass

```python
import jax
import jax.numpy as jnp
from jax import lax
import numpy as np

D_MODEL = 2048
BATCH = 2
SEQ = 4096
DEPTH = 2

CHUNK = 64
Q_BLOCK = 128
SB_HEAD_DIM = 64
SB_WIDTH = D_MODEL // 2
SB_HEADS = SB_WIDTH // SB_HEAD_DIM
CONV_GROUP_DIM = 64
CONV_WIDTH = D_MODEL // 4
CONV_GROUPS = CONV_WIDTH // CONV_GROUP_DIM
CONV_KERNEL = 31
SG_HEAD_DIM = 128
SG_WIDTH = D_MODEL // 4
SG_HEADS = SG_WIDTH // SG_HEAD_DIM
SG_CHUNK = 128
D_MIX = SB_WIDTH + CONV_WIDTH + SG_WIDTH
D_IN_PROJ = 3 * SB_WIDTH + 2 * CONV_WIDTH + 2 * SG_WIDTH
D_FF = ((8 * D_MODEL // 3 + 255) // 256) * 256
FFN_CONV_KERNEL = 3
EPS = 1e-6

kernel_name = 'hybrid_sb_conformer_sgmlp_block'


def rms_norm(x, g):
    xf = x.astype(jnp.float32)
    y = xf * lax.rsqrt(jnp.mean(xf * xf, axis=-1, keepdims=True) + EPS)
    return (y * g.astype(jnp.float32)).astype(x.dtype)


def head_rms_norm(x, n_heads, g):
    shp = x.shape
    y = rms_norm(x.reshape(shp[:-1] + (n_heads, shp[-1] // n_heads)), g.reshape(n_heads, -1))
    return y.reshape(shp)


def group_layer_norm(x, g, b, n_groups):
    shp = x.shape
    xf = x.astype(jnp.float32).reshape(shp[:-1] + (n_groups, shp[-1] // n_groups))
    mu = jnp.mean(xf, axis=-1, keepdims=True)
    var = jnp.mean(jnp.square(xf - mu), axis=-1, keepdims=True)
    y = ((xf - mu) * lax.rsqrt(var + EPS)).reshape(shp)
    return (y * g.astype(jnp.float32) + b.astype(jnp.float32)).astype(x.dtype)


def causal_depthwise_conv(x, w, b):
    k, c = w.shape
    y = lax.conv_general_dilated(
        x, w[:, None, :].astype(x.dtype), window_strides=(1,), padding=[(k - 1, 0)],
        dimension_numbers=('NWC', 'WIO', 'NWC'), feature_group_count=c)
    return y + b.astype(x.dtype)


def stick_breaking_attention(q, k, v):
    s_len, dh = q.shape[1], q.shape[3]
    scale = dh ** -0.5
    outs = []
    for blk in range(s_len // Q_BLOCK):
        q0 = blk * Q_BLOCK
        kend = q0 + Q_BLOCK
        qb = q[:, q0:kend].astype(jnp.float32)
        kb = k[:, :kend].astype(jnp.float32)
        vb = v[:, :kend]
        z = jnp.einsum('bqhd,bkhd->bhqk', qb, kb) * scale
        t_idx = q0 + jnp.arange(Q_BLOCK)[:, None]
        s_idx = jnp.arange(kend)[None, :]
        causal = s_idx < t_idx
        log_beta = jax.nn.log_sigmoid(z)
        log_1m_beta = jnp.where(causal, jax.nn.log_sigmoid(-z), 0.0)
        suffix = lax.cumsum(log_1m_beta, axis=3, reverse=True) - log_1m_beta
        w = jnp.where(causal, jnp.exp(log_beta + suffix), 0.0)
        outs.append(jnp.einsum('bhqk,bkhd->bqhd', w.astype(vb.dtype), vb))
    return jnp.concatenate(outs, axis=1)


def conformer_conv(glu_in, conv_w, conv_b, ln_g, ln_b):
    a, gate = glu_in[..., :CONV_WIDTH], glu_in[..., CONV_WIDTH:]
    h = a * jax.nn.sigmoid(gate)
    h = causal_depthwise_conv(h, conv_w, conv_b)
    h = group_layer_norm(h, ln_g, ln_b, CONV_GROUPS)
    return jax.nn.silu(h)


def chunked_spatial_gating(uv, v_norm, w_s, b_s):
    bsz, s_len, _ = uv.shape
    uv = jax.nn.gelu(uv)
    u, v = uv[..., :SG_WIDTH], uv[..., SG_WIDTH:]
    v = head_rms_norm(v, SG_HEADS, v_norm)
    v = v.reshape(bsz, s_len // SG_CHUNK, SG_CHUNK, SG_HEADS, SG_HEAD_DIM)
    pos_chunk = jnp.arange(SG_CHUNK) // CHUNK
    allowed = pos_chunk[None, :] <= pos_chunk[:, None]
    w = jnp.where(allowed[None], w_s, 0.0).astype(v.dtype)
    mixed = jnp.einsum('hij,bnjhc->bnihc', w, v) + b_s.T.astype(v.dtype)[:, :, None]
    return u * mixed.reshape(bsz, s_len, SG_WIDTH)


def setup_inputs(seed: int = 0) -> dict:
    key = jax.random.key(seed)
    ks = jax.random.split(key, 19)
    f32 = jnp.float32

    def nrm(k, shape, scale):
        return jax.random.normal(k, shape, f32) * scale

    def gain(k, shape):
        return 1.0 + 0.02 * jax.random.normal(k, shape, f32)

    return {
        'x': nrm(ks[0], (BATCH, SEQ, D_MODEL), 1.0),
        'mix_norm': gain(ks[1], (DEPTH, D_MODEL)),
        'w_in': nrm(ks[2], (DEPTH, D_MODEL, D_IN_PROJ), D_MODEL ** -0.5),
        'conv_w': nrm(ks[3], (DEPTH, CONV_KERNEL, CONV_WIDTH), CONV_KERNEL ** -0.5),
        'conv_b': nrm(ks[4], (DEPTH, CONV_WIDTH), 0.02),
        'conv_ln_g': gain(ks[5], (DEPTH, CONV_WIDTH)),
        'conv_ln_b': nrm(ks[6], (DEPTH, CONV_WIDTH), 0.02),
        'sg_v_norm': gain(ks[7], (DEPTH, SG_WIDTH)),
        'sg_w': nrm(ks[8], (DEPTH, SG_HEADS, SG_CHUNK, SG_CHUNK), SG_CHUNK ** -0.5),
        'sg_b': gain(ks[9], (DEPTH, SG_HEADS, SG_CHUNK)),
        'merge_norm': gain(ks[10], (DEPTH, D_MIX)),
        'w_out': nrm(ks[11], (DEPTH, D_MIX, D_MODEL), D_MIX ** -0.5),
        'ffn_norm': gain(ks[12], (DEPTH, D_MODEL)),
        'w_gate': nrm(ks[13], (DEPTH, D_MODEL, D_FF), D_MODEL ** -0.5),
        'w_up': nrm(ks[14], (DEPTH, D_MODEL, D_FF), D_MODEL ** -0.5),
        'ffn_conv_w': nrm(ks[15], (DEPTH, FFN_CONV_KERNEL, D_FF), FFN_CONV_KERNEL ** -0.5),
        'ffn_conv_b': nrm(ks[16], (DEPTH, D_FF), 0.02),
        'w_down': nrm(ks[17], (DEPTH, D_FF, D_MODEL), D_FF ** -0.5),
        'final_norm': gain(ks[18], (D_MODEL,)),
    }


def reference(x, mix_norm, w_in, conv_w, conv_b, conv_ln_g, conv_ln_b, sg_v_norm, sg_w, sg_b,
              merge_norm, w_out, ffn_norm, w_gate, w_up, ffn_conv_w, ffn_conv_b, w_down, final_norm):
    bsz, s_len, _ = x.shape
    o_k = SB_WIDTH
    o_v = 2 * SB_WIDTH
    o_cv = 3 * SB_WIDTH
    o_sg = o_cv + 2 * CONV_WIDTH
    for l in range(DEPTH):
        h = rms_norm(x, mix_norm[l])
        proj = jnp.einsum('bsd,de->bse', h, w_in[l])
        q = proj[..., :o_k].reshape(bsz, s_len, SB_HEADS, SB_HEAD_DIM)
        k = proj[..., o_k:o_v].reshape(bsz, s_len, SB_HEADS, SB_HEAD_DIM)
        v = proj[..., o_v:o_cv].reshape(bsz, s_len, SB_HEADS, SB_HEAD_DIM)
        y_sb = stick_breaking_attention(q, k, v).reshape(bsz, s_len, SB_WIDTH)
        y_cv = conformer_conv(proj[..., o_cv:o_sg], conv_w[l], conv_b[l], conv_ln_g[l], conv_ln_b[l])
        y_sg = chunked_spatial_gating(proj[..., o_sg:], sg_v_norm[l], sg_w[l], sg_b[l])
        g = merge_norm[l]
        y = jnp.concatenate([
            head_rms_norm(y_sb, SB_HEADS, g[:SB_WIDTH]),
            head_rms_norm(y_cv, CONV_GROUPS, g[SB_WIDTH:SB_WIDTH + CONV_WIDTH]),
            head_rms_norm(y_sg, SG_HEADS, g[SB_WIDTH + CONV_WIDTH:]),
        ], axis=-1)
        x = x + jnp.einsum('bse,ed->bsd', y, w_out[l])
        h = rms_norm(x, ffn_norm[l])
        gate = causal_depthwise_conv(jnp.einsum('bsd,df->bsf', h, w_gate[l]), ffn_conv_w[l], ffn_conv_b[l])
        up = jnp.einsum('bsd,df->bsf', h, w_up[l])
        x = x + jnp.einsum('bsf,fd->bsd', jax.nn.silu(gate) * up, w_down[l])
    return rms_norm(x, final_norm)
```

```python
import numpy as np
from contextlib import ExitStack
import concourse.bass as bass
import concourse.mybir as mybir
from concourse.bass_utils import run_bass_kernel_spmd

F32 = mybir.dt.float32
BF16 = mybir.dt.bfloat16
AF = mybir.ActivationFunctionType
ALU = mybir.AluOpType
AX = mybir.AxisListType

D = 2048
T = 1024
NB = 8
DEPTH = 2
DIN = 5120
DFF = 5632
NFC = 44
EPS = 1e-6
P_MIXG, P_CW, P_CB, P_CLG, P_CLB, P_MG, P_FG, P_FW, P_FB, P_FIN, NPP = 0, 16, 140, 144, 148, 152, 168, 184, 316, 360, 376
C_ID, C_MT, C_SEL, C_B0 = 0, 128, 256, 264
CB_AVGD, CB_AVG64, CB_AVG128, CB_NEGU, CB_NEGONE, CB_ZERO, CB_MASK3, NCB = 0, 128, 256, 384, 512, 640, 768, 1280
NCST = C_B0 + NCB


class _Op:
    __slots__ = ("eng", "fn", "deps", "needs_inc", "count", "kind", "chan", "chan_idx", "sem")


class Sched:
    ENGS = ("pe", "act", "dve", "pool", "sp")

    def __init__(self):
        self.streams = {e: [] for e in self.ENGS}
        self.last_w = {}
        self.readers = {}
        self.chan_count = {}
        self.bar_set = []
        self.bar_pending = set()
        self.all_async = []
        self.last_op = {}
        self.n_cc = 0

    def op(self, eng, fn, reads=(), writes=(), kind="c", chan=None):
        o = _Op()
        o.eng, o.fn, o.kind, o.chan = eng, fn, kind, chan
        o.needs_inc = False
        o.count = 0
        o.sem = None
        deps = []
        if eng in self.bar_pending:
            deps.extend(self.bar_set)
            self.bar_pending.discard(eng)
        for k in reads:
            w = self.last_w.get(k)
            if w is not None:
                deps.append(w)
        for k in writes:
            w = self.last_w.get(k)
            if w is not None:
                deps.append(w)
            deps.extend(self.readers.get(k, ()))
        for k in reads:
            self.readers.setdefault(k, []).append(o)
        for k in writes:
            self.last_w[k] = o
            self.readers[k] = []
        seen = set()
        o.deps = []
        for d in deps:
            if id(d) in seen or d is o:
                continue
            seen.add(id(d))
            if d.kind == "c" and d.eng == "pe" and eng == "pe" and kind == "c":
                continue
            o.deps.append(d)
            if d.kind == "c":
                d.needs_inc = True
        if kind == "dma":
            assert chan is not None
            self.chan_count[chan] = self.chan_count.get(chan, 0) + 1
            o.chan_idx = self.chan_count[chan]
            self.all_async.append(o)
        elif kind == "cc":
            self.n_cc += 1
            o.chan = ("cc", self.n_cc)
            self.all_async.append(o)
        else:
            self.last_op[eng] = o
        self.streams[eng].append(o)
        return o

    def barrier(self):
        s = list(self.all_async)
        for e, o in self.last_op.items():
            s.append(o)
            o.needs_inc = True
        self.bar_set = s
        self.bar_pending = set(self.ENGS)
        self.all_async = []

    def emit(self, nc, stack):
        sems = {}
        for e in ("pe", "act", "dve", "pool"):
            sems[e] = stack.enter_context(nc.semaphore("s_" + e))
        chans = {}
        for e in self.ENGS:
            for o in self.streams[e]:
                if o.kind in ("dma", "cc") and o.chan not in chans:
                    chans[o.chan] = stack.enter_context(nc.semaphore("c%d" % len(chans)))
        for e in self.ENGS:
            cnt = 0
            for o in self.streams[e]:
                if o.kind == "c" and o.needs_inc:
                    cnt += 1
                    o.count = cnt
        block = stack.enter_context(nc.Block())

        def run(ename, eng):
            waited = {}
            for o in self.streams[ename]:
                for d in o.deps:
                    if d.kind == "dma":
                        sem, val, key = chans[d.chan], 16 * d.chan_idx, d.chan
                    elif d.kind == "cc":
                        sem, val, key = chans[d.chan], 1, d.chan
                    else:
                        sem, val, key = sems[d.eng], d.count, d.eng
                    if waited.get(key, 0) >= val:
                        continue
                    waited[key] = val
                    eng.wait_ge(sem, val)
                if o.fn is None:
                    continue
                inst = o.fn(eng)
                if o.kind == "dma":
                    inst.then_inc(chans[o.chan], 16)
                elif o.kind == "cc":
                    inst.then_inc(chans[o.chan])
                elif o.needs_inc:
                    inst.then_inc(sems[ename], 1)

        @block.tensor
        def _(eng):
            run("pe", eng)

        @block.scalar
        def _(eng):
            run("act", eng)

        @block.vector
        def _(eng):
            run("dve", eng)

        @block.gpsimd
        def _(eng):
            run("pool", eng)

        @block.sync
        def _(eng):
            run("sp", eng)


def build(dbg_stop=None, ncores=8):
    nc = bass.Bass("TRN2", target_bir_lowering=False)
    x_d = nc.dram_tensor("x", [T, D], F32, kind="ExternalInput").ap()
    w_in_d = nc.dram_tensor("w_in", [DEPTH, D, DIN], F32, kind="ExternalInput").ap()
    w_out_d = nc.dram_tensor("w_out", [DEPTH, D, D], F32, kind="ExternalInput").ap()
    w_gate_d = nc.dram_tensor("w_gate", [DEPTH, D, DFF], F32, kind="ExternalInput").ap()
    w_up_d = nc.dram_tensor("w_up", [DEPTH, D, DFF], F32, kind="ExternalInput").ap()
    w_down_d = nc.dram_tensor("w_down", [DEPTH, DFF, D], F32, kind="ExternalInput").ap()
    pp_d = nc.dram_tensor("pp", [128, DEPTH * NPP], F32, kind="ExternalInput").ap()
    pb_d = nc.dram_tensor("pb", [DEPTH, 128, 1024], F32, kind="ExternalInput").ap()
    ws_d = nc.dram_tensor("ws", [128, DEPTH * 512], F32, kind="ExternalInput").ap()
    cst_d = nc.dram_tensor("cst", [128, NCST], F32, kind="ExternalInput").ap()
    y_d = nc.dram_tensor("y", [T, D], F32, kind="ExternalOutput").ap()
    XK = [nc.dram_tensor("xk%d" % i, [128, 4096], BF16, kind="Internal").ap() for i in range(2)]
    XV = [nc.dram_tensor("xv%d" % i, [128, 4096], BF16, kind="Internal").ap() for i in range(2)]
    GK = [nc.dram_tensor("gk%d" % i, [512, 4096], BF16, kind="Internal").ap() for i in range(2)]
    GV = [nc.dram_tensor("gv%d" % i, [512, 4096], BF16, kind="Internal").ap() for i in range(2)]
    HX = nc.dram_tensor("hx", [128, 960], F32, kind="Internal").ap()
    HG = nc.dram_tensor("hg", [512, 960], F32, kind="Internal").ap()
    XHD = nc.dram_tensor("xhd", [128, 256], F32, kind="Internal").ap()
    XG = nc.dram_tensor("xg", [512, 256], F32, kind="Internal").ap()
    RG = [[0, 1, 2, 3], [4, 5, 6, 7]] if ncores == 8 else [[0, 1, 2, 3]]

    S = Sched()
    st = ExitStack()
    with st:
        def sb(name, shape, dt):
            return st.enter_context(nc.sbuf_tensor("sb_" + name, shape, dt))

        t_xT = sb("xT", [128, 16 * T], F32)
        t_hT = sb("hT", [128, 16 * T], BF16)
        t_qs = sb("qs", [128, 8 * T], BF16)
        t_ysg = sb("ysg", [128, 4 * T], BF16)
        t_hp = sb("hp", [128, 5056], F32)
        t_S = sb("S", [128, 8960], F32)
        t_pp = sb("pp", [128, DEPTH * NPP], F32)
        t_pb = sb("pb", [128, 1024], F32)
        t_ws = sb("ws", [128, DEPTH * 512], F32)
        t_wsm = sb("wsm", [128, 512], BF16)
        t_cst = sb("cst", [128, NCST], F32)
        t_cb = sb("cb", [128, NCB], BF16)
        ps = [st.enter_context(nc.psum_tensor("ps%d" % i, [128, 512], F32)) for i in range(8)]

        xT = t_xT[:, :].rearrange("p (c t) -> p c t", c=16)
        hT = t_hT[:, :].rearrange("p (c t) -> p c t", c=16)
        qs = t_qs[:, :].rearrange("p (c t) -> p c t", c=8)
        ysg = t_ysg[:, :].rearrange("p (c t) -> p c t", c=4)
        hp4 = t_hp[:, :].rearrange("p (c b t) -> p c b t", c=4, b=8)
        hp_cb = t_hp[:, :].rearrange("p (cb t) -> p cb t", t=158)

        def Sf(off, n):
            return t_S[:, off:off + n]

        def Sb(off, n):
            return t_S[:, off:off + n].bitcast(BF16)

        def Hf(off, n):
            return t_hp[:, off:off + n]

        def Hb(off, n):
            return t_hp[:, off:off + n].bitcast(BF16)

        ident = t_cst[:, C_ID:C_ID + 128]
        maskT = t_cst[:, C_MT:C_MT + 128]

        def sel(r):
            return t_cst[:, C_SEL + r:C_SEL + r + 1]

        avgD = t_cb[:, CB_AVGD:CB_AVGD + 128]
        avg64 = t_cb[:, CB_AVG64:CB_AVG64 + 128]
        avg128 = t_cb[:, CB_AVG128:CB_AVG128 + 128]
        negU = t_cb[:, CB_NEGU:CB_NEGU + 128]
        negone = t_cb[:, CB_NEGONE:CB_NEGONE + 128]
        zeros = t_cb[:, CB_ZERO:CB_ZERO + 128]
        mask3 = t_cb[:, CB_MASK3:CB_MASK3 + 512].rearrange("p (r q) -> p r q", r=4)

        def ppc(l, col, n=1):
            return t_pp[:, l * NPP + col:l * NPP + col + n]

        bank_ctr = [0]

        def nb():
            b = bank_ctr[0] % 8
            bank_ctr[0] += 1
            return b

        def mm(out, lhsT, rhs, start, stop, reads, writes, skip=False):
            if skip:
                S.op("pe", lambda e: e.matmul(out, lhsT, rhs, start=start, stop=stop, skip_group_check=True), reads, writes)
            else:
                S.op("pe", lambda e: e.matmul(out, lhsT, rhs, start=start, stop=stop), reads, writes)

        def act(out, in_, func, reads, writes, bias=None, scale=None):
            kw = {}
            if bias is not None:
                kw["bias"] = bias
            if scale is not None:
                kw["scale"] = scale
            S.op("act", lambda e: e.activation(out=out, in_=in_, func=func, **kw), reads, writes)

        def tt(eng, out, in0, in1, op, reads, writes):
            S.op(eng, lambda e: e.tensor_tensor(out=out, in0=in0, in1=in1, op=op), reads, writes)

        def ts(out, in0, s1, s2, op0, op1, reads, writes):
            if op1 is None:
                S.op("dve", lambda e: e.tensor_scalar(out=out, in0=in0, scalar1=s1, scalar2=None, op0=op0), reads, writes)
            else:
                S.op("dve", lambda e: e.tensor_scalar(out=out, in0=in0, scalar1=s1, scalar2=s2, op0=op0, op1=op1), reads, writes)

        def stt(out, in0, scalar, in1, op0, op1, reads, writes):
            S.op("dve", lambda e: e.scalar_tensor_tensor(out=out, in0=in0, scalar=scalar, in1=in1, op0=op0, op1=op1), reads, writes)

        def rs(out, in_, reads, writes):
            act(out, in_, AF.Ln, reads, writes, bias=EPS)
            act(out, out, AF.Exp, writes, writes, scale=-0.5)

        def cp(eng, out, in_, reads, writes):
            S.op(eng, lambda e: e.tensor_copy(out=out, in_=in_), reads, writes)

        def dma(q, out, in_, reads, writes, chan):
            S.op(q, lambda e: e.dma_start(out=out, in_=in_), reads, writes, kind="dma", chan=chan)

        def xk(c):
            return [("xT", c, 0), ("xT", c, 1)]

        def hk(c):
            return [("hT", c, 0), ("hT", c, 1)]

        dma("sp", t_cst[:, :], cst_d, [], ["cst"], "cst")
        dma("sp", t_pp[:, :], pp_d, [], ["pp"], "pp")
        dma("sp", t_ws[:, :], ws_d, [], ["ws"], "ws")
        cp("dve", t_cb[:, :], t_cst[:, C_B0:C_B0 + NCB], ["cst"], ["cb"])
        for i in range(NB):
            xin = Sf((i % 2) * 2048, 2048)
            dma("sp", xin, x_d[i * 128:(i + 1) * 128, :], [], [("xin", i % 2)], ("xin", i % 2))
            for cg in range(4):
                b = nb()
                for cc in range(4):
                    c = cg * 4 + cc
                    S.op("pe", lambda e, b=b, cc=cc, c=c, xin=xin: e.transpose(ps[b][:, cc * 128:(cc + 1) * 128], xin[:, c * 128:(c + 1) * 128], ident),
                         [("xin", i % 2), "cst"], [("ps", b)])
                eng = "act" if (cg % 2 == 0) else "dve"
                outv = xT[:, cg * 4:cg * 4 + 4, i * 128:(i + 1) * 128]
                inv = ps[b][:, :].rearrange("p (a t) -> p a t", a=4)
                if eng == "act":
                    act(outv, inv, AF.Copy, [("ps", b)], [("xT", cg * 4 + a, i // 4) for a in range(4)])
                else:
                    cp("dve", outv, inv, [("ps", b)], [("xT", cg * 4 + a, i // 4) for a in range(4)])

        def norm_to_hT(l, gcol):
            S.barrier()
            rstd = Sf(0, 1024)
            b0, b1 = nb(), nb()
            bb = (b0, b1)
            for c in range(16):
                xsq = Sb(1024 + (c % 2) * 512, 512)
                act(xsq, xT[:, c, :], AF.Square, xk(c), [("xsq", c % 2)])
                for h in range(2):
                    mm(ps[bb[h]][:, :], avgD, xsq[:, h * 512:(h + 1) * 512], c == 0, c == 15,
                       [("xsq", c % 2), "cb"], [("ps", bb[h])])
            for h in range(2):
                rs(rstd[:, h * 512:(h + 1) * 512], ps[bb[h]][:, :], [("ps", bb[h])], [("rstd", h)])
            return rstd

        def apply_norm(l, gcol, rstd):
            for c in range(16):
                for h in range(2):
                    stt(hT[:, c, h * 512:(h + 1) * 512], xT[:, c, h * 512:(h + 1) * 512], ppc(l, gcol + c),
                        rstd[:, h * 512:(h + 1) * 512], ALU.mult, ALU.mult,
                        [("xT", c, h), ("rstd", h), "pp"], [("hT", c, h)])

        def load_slab(wd2, col0, view, key):
            src = wd2[:, col0:col0 + 256].rearrange("(k p) n -> p k n", p=128)
            dma("pool", view, src, [], [key], key)

        def proj_fm(slab, slabkey, lc, banks, rhs_of, rkeys_of, nk=16):
            for k in range(nk):
                for h in range(2):
                    mm(ps[banks[h]][:, :], slab[:, k, lc * 128:(lc + 1) * 128], rhs_of(k, h), k == 0, k == nk - 1,
                       [slabkey] + rkeys_of(k, h), [("ps", banks[h])])

        def hT_rhs(k, h):
            return hT[:, k, h * 512:(h + 1) * 512]

        def hT_keys(k, h):
            return [("hT", k, h)]

        def head_norm_write(ypre, ykeys, avg, sqv, sqkey, rsv, rskey, gcolap, out_of, outkeys_of):
            act(sqv, ypre, AF.Square, ykeys, [sqkey])
            for h in range(2):
                b = nb()
                mm(ps[b][:, :], avg, sqv[:, h * 512:(h + 1) * 512], True, True, [sqkey, "cb"], [("ps", b)])
                rs(rsv[:, h * 512:(h + 1) * 512], ps[b][:, :], [("ps", b)], [(rskey, h)])
                stt(out_of(h), ypre[:, h * 512:(h + 1) * 512], gcolap, rsv[:, h * 512:(h + 1) * 512], ALU.mult, ALU.mult,
                    ykeys + [(rskey, h), "pp"], outkeys_of(h))

        for l in range(DEPTH):
            w_in_l = w_in_d[l]
            rstd = norm_to_hT(l, P_MIXG)
            apply_norm(l, P_MIXG, rstd)
            dma("sp", t_pb[:, :], pb_d[l], [], ["pb"], "pb")
            S.barrier()
            vn = Sb(0, 2048).rearrange("p (i f) -> p i f", i=8)
            wsl = [Sb(2048, 2048).rearrange("p (k n) -> p k n", k=16), Sb(4096, 2048).rearrange("p (k n) -> p k n", k=16)]
            tA = Sf(6144, 1024)
            tBf = Sf(7168, 512)
            tBb = Sb(7168, 512)
            stg = [Sb(7680, 128), Sb(7808, 128)]
            yp = Sf(7936, 1024)
            slab_i = [0]

            pre = {}

            def next_slab(col0):
                if col0 in pre:
                    return pre.pop(col0)
                i = slab_i[0] % 2
                slab_i[0] += 1
                load_slab(w_in_l, col0, wsl[i], ("wsl", i))
                return wsl[i], ("wsl", i)

            def preload(col0):
                pre[col0] = next_slab(col0)

            def allgather(src, dst, rkeys, wkeys):
                S.op("pool", lambda e: e.collective_compute("AllGather", ALU.bypass, replica_groups=RG, ins=[src], outs=[dst]),
                     rkeys, wkeys, kind="cc")

            for h4 in range(4):
                tt("dve", t_wsm[:, h4 * 128:(h4 + 1) * 128], t_ws[:, l * 512 + h4 * 128:l * 512 + (h4 + 1) * 128], maskT, ALU.mult,
                   ["ws", "cst"], [("wsm", h4)])
            vnb = t_pb[:, 0:512]
            bbc = t_pb[:, 512:1024]
            for s2 in range(2):
                slab, skey = next_slab(4608 + s2 * 256)
                for i in range(NB):
                    b = nb()
                    for k in range(16):
                        mm(ps[b][:, 0:256], hT[:, k, i * 128:(i + 1) * 128], slab[:, k, :], k == 0, k == 15,
                           [skey, ("hT", k, i // 4)], [("ps", b)])
                    vg = tBf[:, 0:256]
                    sqt = tBf[:, 256:512]
                    act(vg, ps[b][:, 0:256], AF.Gelu, [("ps", b)], ["tB"])
                    tt("dve", sqt, vg, vg, ALU.mult, ["tB"], ["tB"])
                    ssq = tA[:, 0:2]
                    S.op("dve", lambda e, ssq=ssq, sqt=sqt: e.reduce_sum(out=ssq, in_=sqt.rearrange("p (a f) -> p a f", a=2), axis=AX.X),
                         ["tB"], [("uT", 0)])
                    ts(tA[:, 2:4], ssq, 1.0 / 128.0, EPS, ALU.mult, ALU.add, [("uT", 0)], [("uT", 0)])
                    act(tA[:, 4:6], tA[:, 2:4], AF.Ln, [("uT", 0)], [("uT", 0)])
                    act(tA[:, 4:6], tA[:, 4:6], AF.Exp, [("uT", 0)], [("uT", 0)], scale=-0.5)
                    for hh in range(2):
                        h4 = s2 * 2 + hh
                        stt(vn[:, i, h4 * 128:(h4 + 1) * 128], vg[:, hh * 128:(hh + 1) * 128], tA[:, 4 + hh:5 + hh],
                            vnb[:, h4 * 128:(h4 + 1) * 128], ALU.mult, ALU.mult, ["tB", ("uT", 0), "pb"], [("vn", i, h4)])
            for s2 in range(2):
                slab, skey = next_slab(4096 + s2 * 256)
                for lc in range(2):
                    h4 = s2 * 2 + lc
                    bk = (nb(), nb())
                    proj_fm(slab, skey, lc, bk, hT_rhs, hT_keys)
                    for h in range(2):
                        act(tA[:, h * 512:(h + 1) * 512], ps[bk[h]][:, :], AF.Gelu, [("ps", bk[h])], [("uT", h)])
                    for h in range(2):
                        b = nb()
                        for u in range(4):
                            i = h * 4 + u
                            mm(ps[b][:, u * 128:(u + 1) * 128], vn[:, i, h4 * 128:(h4 + 1) * 128], t_wsm[:, h4 * 128:(h4 + 1) * 128],
                               True, True, [("vn", i, h4), ("wsm", h4)], [("ps", b)])
                        for u in range(4):
                            i = h * 4 + u
                            tt("dve", yp[:, i * 128:(i + 1) * 128], ps[b][:, u * 128:(u + 1) * 128], bbc[:, h4 * 128:(h4 + 1) * 128], ALU.add,
                               [("ps", b), "pb"], [("yp", i)])
                            tt("dve", yp[:, i * 128:(i + 1) * 128], yp[:, i * 128:(i + 1) * 128], tA[:, i * 128:(i + 1) * 128], ALU.mult,
                               [("yp", i), ("uT", h)], [("yp", i)])
                    ypk = [("yp", i) for i in range(8)]
                    head_norm_write(yp, ypk, avg128, tBb, "tB", tA, "uT", ppc(l, P_MG + 12 + h4),
                                    lambda h, h4=h4: ysg[:, h4, h * 512:(h + 1) * 512], lambda h, h4=h4: [("ysg", h4, h)])
            for s2 in range(2):
                slab_a, ka = next_slab(3072 + s2 * 256)
                slab_g, kg = next_slab(3584 + s2 * 256)
                for lc in range(2):
                    c = s2 * 2 + lc
                    ba = (nb(), nb())
                    proj_fm(slab_a, ka, lc, ba, hT_rhs, hT_keys)
                    bg = (nb(), nb())
                    proj_fm(slab_g, kg, lc, bg, hT_rhs, hT_keys)
                    for h in range(2):
                        act(tA[:, h * 512:(h + 1) * 512], ps[bg[h]][:, :], AF.Sigmoid, [("ps", bg[h])], [("uT", h)])
                        tt("dve", hp4[:, c, h * 4:(h + 1) * 4, 30:158], ps[ba[h]][:, :].rearrange("p (a t) -> p a t", a=4),
                           tA[:, h * 512:(h + 1) * 512].rearrange("p (a t) -> p a t", a=4), ALU.mult,
                           [("ps", ba[h]), ("uT", h)], [("hp", c)])
            for s4 in range(4):
                slab, skey = next_slab(1024 + s4 * 256)
                for lc in range(2):
                    c = s4 * 2 + lc
                    bk = (nb(), nb())
                    proj_fm(slab, skey, lc, bk, hT_rhs, hT_keys)
                    for h in range(2):
                        act(tBb[:, h * 512:(h + 1) * 512], ps[bk[h]][:, :], AF.Copy, [("ps", bk[h])], ["tB"])
                    dma("sp", XK[c // 4][:, (c % 4) * 1024:(c % 4 + 1) * 1024], tBb, ["tB"], [("XK", c)], "tBst")
            preload(2048)
            for half in range(2):
                allgather(XK[half], GK[half], [("XK", half * 4 + a) for a in range(4)], [("GK", half)])
            xv4 = [XV[a].rearrange("p (c i f) -> p c i f", c=4, i=8) for a in range(2)]
            vi = 0
            for s4 in range(4):
                slab, skey = next_slab(2048 + s4 * 256)
                for i in range(NB):
                    b = nb()
                    for k in range(16):
                        mm(ps[b][:, 0:256], hT[:, k, i * 128:(i + 1) * 128], slab[:, k, :], k == 0, k == 15,
                           [skey, ("hT", k, i // 4)], [("ps", b)])
                    sg_ = stg[vi % 2]
                    if vi % 2 == 0:
                        act(sg_, ps[b][:, 0:256], AF.Copy, [("ps", b)], [("stg", vi % 2)])
                    else:
                        cp("dve", sg_, ps[b][:, 0:256], [("ps", b)], [("stg", vi % 2)])
                    dma("sp", xv4[s4 // 2][:, (s4 % 2) * 2:(s4 % 2) * 2 + 2, i, :], sg_.rearrange("p (c f) -> p c f", c=2), [("stg", vi % 2)],
                        [("XV", s4, i)], ("stg", vi % 2))
                    vi += 1
            preload(0)
            for half in range(2):
                allgather(XV[half], GV[half], [("XV", half * 2 + a, i) for a in range(2) for i in range(8)], [("GV", half)])
            dma("sp", HX.rearrange("p (cb t) -> p cb t", t=30), hp_cb[:, :, 128:158], [("hp", c) for c in range(4)], ["HX"], "HXst")
            allgather(HX, HG, ["HX"], ["HG"])
            for s4 in range(4):
                slab, skey = next_slab(s4 * 256)
                for lc in range(2):
                    c = s4 * 2 + lc
                    bk = (nb(), nb())
                    proj_fm(slab, skey, lc, bk, hT_rhs, hT_keys)
                    for h in range(2):
                        act(qs[:, c, h * 512:(h + 1) * 512], ps[bk[h]][:, :], AF.Identity, [("ps", bk[h])], [("qs", c)], scale=0.125)
            S.barrier()
            hg = Sf(0, 3840)
            hg_r = hg.rearrange("p (r cb t) -> p r cb t", r=4, t=30)
            hg_rc = hg.rearrange("p (r c b t) -> p r c b t", r=4, c=4, t=30)
            acc = Sf(3840, 1024)
            xc = Sf(4864, 1024)
            sqb = Sb(5888, 512)
            dma("sp", hg.rearrange("p (r n) -> p r n", r=4), HG.rearrange("(r p) n -> p r n", p=128), ["HG"], ["hg"], "hg")
            hv = hp_cb[:, :, 0:30]
            hpk = [("hp", c) for c in range(4)]
            ts(hv, hg_r[:, 0], sel(0), None, ALU.mult, None, ["hg", "cst"], hpk)
            for r in (1, 2):
                stt(hv, hg_r[:, r], sel(r), hv, ALU.mult, ALU.add, ["hg", "cst"] + hpk, hpk)
            for c in range(4):
                stt(hp4[:, c, 1:8, 0:30], hg_rc[:, 3, c, 0:7, :], sel(3), hp4[:, c, 1:8, 0:30], ALU.mult, ALU.add,
                    ["hg", "cst", ("hp", c)], [("hp", c)])
            for c in range(4):
                acc3 = acc.rearrange("p (b t) -> p b t", b=8)
                acck = [("acc", 0), ("acc", 1)]
                act(acc3, hp4[:, c, :, 0:128], AF.Identity, [("hp", c), "pp"], acck,
                    bias=ppc(l, P_CB + c), scale=ppc(l, P_CW + c * 31))
                for k in range(1, 31):
                    stt(acc3, hp4[:, c, :, k:k + 128], ppc(l, P_CW + c * 31 + k), acc3, ALU.mult, ALU.add,
                        [("hp", c), "pp"] + acck, acck)
                act(sqb, acc, AF.Copy, acck, ["sqb"])
                for h in range(2):
                    b = nb()
                    mm(ps[b][:, :], avg64, sqb[:, h * 512:(h + 1) * 512], True, True, ["sqb", "cb"], [("ps", b)])
                    tt("dve", xc[:, h * 512:(h + 1) * 512], acc[:, h * 512:(h + 1) * 512], ps[b][:, :], ALU.subtract,
                       [("acc", h), ("ps", b)], [("xc", h)])
                act(sqb, xc, AF.Square, [("xc", 0), ("xc", 1)], ["sqb"])
                for h in range(2):
                    b = nb()
                    mm(ps[b][:, :], avg64, sqb[:, h * 512:(h + 1) * 512], True, True, ["sqb", "cb"], [("ps", b)])
                    rs(acc[:, h * 512:(h + 1) * 512], ps[b][:, :], [("ps", b)], [("acc", h)])
                    tt("dve", xc[:, h * 512:(h + 1) * 512], xc[:, h * 512:(h + 1) * 512], acc[:, h * 512:(h + 1) * 512], ALU.mult,
                       [("xc", h), ("acc", h)], [("xc", h)])
                act(xc, xc, AF.Silu, [("xc", 0), ("xc", 1), "pp"], [("xc", 0), ("xc", 1)],
                    bias=ppc(l, P_CLB + c), scale=ppc(l, P_CLG + c))
                head_norm_write(xc, [("xc", 0), ("xc", 1)], avg64, sqb, "sqb", acc, "acc", ppc(l, P_MG + 8 + c),
                                lambda h, c=c: hT[:, 8 + c, h * 512:(h + 1) * 512], lambda h, c=c: [("hT", 8 + c, h)])
            S.barrier()
            kvb = [Sb(0, 4096), Sb(4096, 4096)]
            gk_r = [GK[a].rearrange("(r p) n -> p r n", p=128) for a in range(2)]
            gv_r = [GV[a].rearrange("(r p) n -> p r n", p=128) for a in range(2)]
            elw = [Hf(0, 512), Hf(1536, 512)]
            spb = [Hb(512, 256), Hb(2048, 256)]
            wb = [Hb(768, 256), Hb(2304, 256)]
            ncar = [Hf(1024, 512), Hf(2560, 512)]
            ypre = Hf(3072, 1024)
            sqa = Hb(4096, 512)
            bA = (nb(), nb())
            bB = (nb(), nb())
            bO = (nb(), nb())
            bC = nb()
            for c in range(8):
                buf = c % 2
                KT = kvb[buf][:, 0:4096].rearrange("p (r t) -> p r t", r=4)
                VT = kvb[buf][:, 4096:8192].rearrange("p (r i f) -> p r i f", r=4, i=8)
                dma("sp", KT, gk_r[c // 4][:, :, (c % 4) * 1024:(c % 4 + 1) * 1024], [("GK", c // 4)], [("kvK", buf)], ("kvK", buf))
                dma("sp", kvb[buf][:, 4096:8192].rearrange("p (r t) -> p r t", r=4), gv_r[c // 4][:, :, (c % 4) * 1024:(c % 4 + 1) * 1024],
                    [("GV", c // 4)], [("kvV", buf)], ("kvV", buf))
                for g in range(2):
                    nkb = 16 * (g + 1)
                    for hh in range(2):
                        S.op("pool", lambda e, hh=hh: e.memset(ncar[hh], 0.0), [], [("ncar", hh)])
                        mm(ps[bO[hh]][:, :], zeros, qs[:, c, 0:512], True, False, ["cb", ("qs", c)], [("ps", bO[hh])], skip=True)
                    for kb in range(nkb - 1, -1, -1):
                        r = kb % 4
                        il = kb // 4
                        u0 = max(0, il - 4 * g)
                        edge = il >= 4 * g
                        c0 = u0 * 128
                        W = 512 - c0
                        for hh in range(2):
                            hb = 64 * hh
                            kT_l = KT[hb:hb + 64, r, il * 128:(il + 1) * 128]
                            q_r = qs[hb:hb + 64, c, g * 512 + c0:(g + 1) * 512]
                            A = ps[bA[hh]][:, c0:512]
                            B = ps[bB[hh]][:, c0:512]
                            Cc = ps[bC][:, c0:512]
                            O = ps[bO[hh]][:, c0:512]
                            e_ = elw[hh][:, c0:512]
                            sp_ = spb[hh][:, c0:512]
                            w_ = wb[hh][:, c0:512]
                            nc_ = ncar[hh][:, c0:512]
                            mk = mask3[:, r, :]
                            mm(A, kT_l, q_r, True, True, [("kvK", buf), ("qs", c)], [("ps", bA[hh])])
                            act(e_, A, AF.Exp, [("ps", bA[hh])], [("elw", hh)])
                            act(sp_, e_, AF.Ln, [("elw", hh)], [("sp", hh)], bias=1.0)
                            if edge:
                                tt("pool", sp_[:, 0:128], sp_[:, 0:128], mk, ALU.mult, [("sp", hh), "cb"], [("sp", hh)])
                            mm(B, negU, sp_, True, False, ["cb", ("sp", hh)], [("ps", bB[hh])])
                            mm(B, kT_l, q_r, False, True, [("kvK", buf), ("qs", c)], [("ps", bB[hh])])
                            mm(Cc, negone, sp_, True, True, ["cb", ("sp", hh)], [("ps", bC)])
                            tt("dve", e_, B, nc_, ALU.add, [("ps", bB[hh]), ("ncar", hh)], [("elw", hh)])
                            act(w_, e_, AF.Exp, [("elw", hh)], [("w", hh)])
                            if edge:
                                tt("pool", w_[:, 0:128], w_[:, 0:128], mk, ALU.mult, [("w", hh), "cb"], [("w", hh)])
                            tt("dve", nc_, Cc, nc_, ALU.add, [("ps", bC), ("ncar", hh)], [("ncar", hh)])
                            mm(O, VT[:, r, il, :], w_, False, kb == 0, [("kvV", buf), ("w", hh)], [("ps", bO[hh])], skip=True)
                    for hh in range(2):
                        hb = 64 * hh
                        act(ypre[hb:hb + 64, g * 512:(g + 1) * 512], ps[bO[hh]][hb:hb + 64, :], AF.Copy,
                            [("ps", bO[hh])], [("ypre", g)])
                act(sqa, ypre, AF.Square, [("ypre", 0), ("ypre", 1)], ["sqa"])
                for h in range(2):
                    b = nb() if False else bC
                    mm(ps[b][:, :], avg64, sqa[:, h * 512:(h + 1) * 512], True, True, ["sqa", "cb"], [("ps", b)])
                    rs(elw[h], ps[b][:, :], [("ps", b)], [("elw", h)])
                    stt(hT[:, c, h * 512:(h + 1) * 512], ypre[:, h * 512:(h + 1) * 512], ppc(l, P_MG + c), elw[h], ALU.mult, ALU.mult,
                        [("ypre", h), ("elw", h), "pp"], [("hT", c, h)])
            S.barrier()
            wsl = [Sb(0, 2048).rearrange("p (k n) -> p k n", k=16), Sb(2048, 2048).rearrange("p (k n) -> p k n", k=16)]

            def y_rhs(k, h):
                if k < 12:
                    return hT[:, k, h * 512:(h + 1) * 512]
                return ysg[:, k - 12, h * 512:(h + 1) * 512]

            def y_keys(k, h):
                if k < 12:
                    return [("hT", k, h)]
                return [("ysg", k - 12, h)]

            for s8 in range(8):
                i2 = s8 % 2
                load_slab(w_out_d[l], s8 * 256, wsl[i2], ("wsl", i2))
                for lc in range(2):
                    dc = s8 * 2 + lc
                    bk = (nb(), nb())
                    proj_fm(wsl[i2], ("wsl", i2), lc, bk, y_rhs, y_keys)
                    for h in range(2):
                        tt("dve", xT[:, dc, h * 512:(h + 1) * 512], ps[bk[h]][:, :], xT[:, dc, h * 512:(h + 1) * 512], ALU.add,
                           [("ps", bk[h]), ("xT", dc, h)], [("xT", dc, h)])
            if dbg_stop == ("mix", l):
                break
            S.barrier()
            xh_st = Hf(0, 256)
            xgt = Hf(256, 1024)
            xh = Hf(1280, 256)
            sqh = Hb(1536, 128)
            h2h = Hb(1664, 128)
            rsh = Hf(1792, 16)
            gpb = [Hf(1824, 520), Hf(2344, 520)]
            cbuf = [Hf(2864, 512), Hf(3376, 512)]
            xT_bct = t_xT[:, :].rearrange("p (c b t) -> p b c t", c=16, b=8)
            cp("dve", xh_st.rearrange("p (b c t) -> p b c t", b=8, c=16), xT_bct[:, :, :, 126:128],
               [k for c in range(16) for k in xk(c)], ["xh_st"])
            dma("sp", XHD, xh_st, ["xh_st"], ["XHD"], "XHDst")
            S.op("pool", lambda e: e.collective_compute("AllGather", ALU.bypass, replica_groups=RG, ins=[XHD], outs=[XG]),
                 ["XHD"], ["XG"], kind="cc")
            dma("sp", xgt.rearrange("p (r n) -> p r n", r=4), XG.rearrange("(r p) n -> p r n", p=128), ["XG"], ["xgt"], "xgt")
            xg_r = xgt.rearrange("p (r n) -> p r n", r=4)
            ts(xh, xg_r[:, 0], sel(0), None, ALU.mult, None, ["xgt", "cst"], ["xh"])
            for r in (1, 2):
                stt(xh, xg_r[:, r], sel(r), xh, ALU.mult, ALU.add, ["xgt", "cst", "xh"], ["xh"])
            stt(xh[:, 32:256], xg_r[:, 3, 0:224], sel(3), xh[:, 32:256], ALU.mult, ALU.add, ["xgt", "cst", "xh"], ["xh"])
            xh_bct = xh.rearrange("p (b c t) -> p b c t", b=8, c=16)
            sqh_cbt = sqh.rearrange("p (c b t) -> p c b t", c=16, b=8)
            h2h_cbt = h2h.rearrange("p (c b t) -> p c b t", c=16, b=8)
            h2h_c = h2h.rearrange("p (c n) -> p c n", c=16)
            sqh_c = sqh.rearrange("p (c n) -> p c n", c=16)
            bS = nb()
            for dc in range(16):
                act(sqh_cbt[:, dc], xh_bct[:, :, dc, :], AF.Square, ["xh"], [("sqh", dc)])
                mm(ps[bS][:, 0:16], avgD, sqh_c[:, dc, :], dc == 0, dc == 15, [("sqh", dc), "cb"], [("ps", bS)])
            rs(rsh, ps[bS][:, 0:16], [("ps", bS)], ["rsh"])
            for dc in range(16):
                stt(h2h_cbt[:, dc], xh_bct[:, :, dc, :], ppc(l, P_FG + dc), rsh.rearrange("p (b t) -> p b t", b=8), ALU.mult, ALU.mult,
                    ["xh", "rsh", "pp"], [("h2h", dc)])
            rstd = norm_to_hT(l, P_FG)
            apply_norm(l, P_FG, rstd)
            S.barrier()
            wg = [Sb(0, 2048).rearrange("p (k n) -> p k n", k=16), Sb(2048, 2048).rearrange("p (k n) -> p k n", k=16)]
            wu = [Sb(4096, 2048).rearrange("p (k n) -> p k n", k=16), Sb(6144, 2048).rearrange("p (k n) -> p k n", k=16)]
            wd = [t_qs[:, 0:4096].rearrange("p (f d) -> p f d", f=2), t_qs[:, 4096:8192].rearrange("p (f d) -> p f d", f=2)]
            aT = [t_ysg[:, 0:2048].rearrange("p (f t) -> p f t", f=2), t_ysg[:, 2048:4096].rearrange("p (f t) -> p f t", f=2)]
            bG = (nb(), nb())
            bU = (nb(), nb())
            bH = nb()
            bD = [nb(), nb(), nb()]
            dctr = 0
            ectr = 0
            for grp in range(22):
                buf = grp % 2
                load_slab(w_gate_d[l], grp * 256, wg[buf], ("wg", buf))
                load_slab(w_up_d[l], grp * 256, wu[buf], ("wu", buf))
                dma("pool", wd[buf], w_down_d[l][grp * 256:(grp + 1) * 256, :].rearrange("(f p) d -> p f d", p=128),
                    [], [("wd", buf)], ("wd", buf))
                for fl in range(2):
                    fc = grp * 2 + fl
                    for k in range(16):
                        for h in range(2):
                            mm(ps[bG[h]][:, :], wg[buf][:, k, fl * 128:(fl + 1) * 128], hT[:, k, h * 512:(h + 1) * 512], k == 0, k == 15,
                               [("wg", buf), ("hT", k, h)], [("ps", bG[h])])
                        mm(ps[bH][:, 0:16], wg[buf][:, k, fl * 128:(fl + 1) * 128], h2h_c[:, k, :], k == 0, k == 15,
                           [("wg", buf), ("h2h", k)], [("ps", bH)])
                    for k in range(16):
                        for h in range(2):
                            mm(ps[bU[h]][:, :], wu[buf][:, k, fl * 128:(fl + 1) * 128], hT[:, k, h * 512:(h + 1) * 512], k == 0, k == 15,
                               [("wu", buf), ("hT", k, h)], [("ps", bU[h])])
                    for h in range(2):
                        e2 = ectr % 2
                        ectr += 1
                        gp3 = gpb[e2].rearrange("p (b t) -> p b t", b=4)
                        cb3 = cbuf[e2].rearrange("p (b t) -> p b t", b=4)
                        G3 = ps[bG[h]][:, :].rearrange("p (b t) -> p b t", b=4)
                        act(gp3[:, :, 2:130], G3, AF.Copy, [("ps", bG[h])], [("gp", e2)])
                        cp("dve", gp3[:, :, 0:2], ps[bH][:, h * 8:(h + 1) * 8].rearrange("p (b t) -> p b t", b=4), [("ps", bH)], [("gp", e2)])
                        act(cb3, G3, AF.Identity, [("ps", bG[h]), "pp"], [("cbuf", e2)],
                            bias=ppc(l, P_FB + fc), scale=ppc(l, P_FW + fc * 3 + 2))
                        stt(cb3, gp3[:, :, 1:129], ppc(l, P_FW + fc * 3 + 1), cb3, ALU.mult, ALU.add, [("gp", e2), ("cbuf", e2), "pp"], [("cbuf", e2)])
                        stt(cb3, gp3[:, :, 0:128], ppc(l, P_FW + fc * 3 + 0), cb3, ALU.mult, ALU.add, [("gp", e2), ("cbuf", e2), "pp"], [("cbuf", e2)])
                        act(cbuf[e2], cbuf[e2], AF.Silu, [("cbuf", e2)], [("cbuf", e2)])
                        tt("dve", aT[buf][:, fl, h * 512:(h + 1) * 512], cbuf[e2], ps[bU[h]][:, :], ALU.mult,
                           [("cbuf", e2), ("ps", bU[h])], [("aT", buf, fl, h)])
                for dc in range(16):
                    for h in range(2):
                        b = bD[dctr % 3]
                        dctr += 1
                        for fl in range(2):
                            mm(ps[b][:, :], wd[buf][:, fl, dc * 128:(dc + 1) * 128], aT[buf][:, fl, h * 512:(h + 1) * 512], fl == 0, fl == 1,
                               [("wd", buf), ("aT", buf, fl, h)], [("ps", b)])
                        tt("dve", xT[:, dc, h * 512:(h + 1) * 512], ps[b][:, :], xT[:, dc, h * 512:(h + 1) * 512], ALU.add,
                           [("ps", b), ("xT", dc, h)], [("xT", dc, h)])
            if dbg_stop == ("ffn", l):
                break

        if dbg_stop is None:
            rstd = norm_to_hT(DEPTH - 1, P_FIN)
        else:
            S.barrier()
            rstd = None
        onT = Sf(2048, 4096).rearrange("p (c t) -> p c t", c=4)
        ost = [Sf(6144, 512), Sf(6656, 512), Sf(7168, 512), Sf(7680, 512)]
        octr = 0
        for cg in range(4):
            for a in range(4):
                c = cg * 4 + a
                if rstd is not None:
                    for h in range(2):
                        stt(onT[:, a, h * 512:(h + 1) * 512], xT[:, c, h * 512:(h + 1) * 512], ppc(DEPTH - 1, P_FIN + c),
                            rstd[:, h * 512:(h + 1) * 512], ALU.mult, ALU.mult, [("xT", c, h), ("rstd", h), "pp"], [("onT", a)])
                else:
                    cp("dve", onT[:, a, :], xT[:, c, :], xk(c), [("onT", a)])
            for i in range(NB):
                b = nb()
                for a in range(4):
                    S.op("pe", lambda e, b=b, a=a, i=i: e.transpose(ps[b][:, a * 128:(a + 1) * 128], onT[:, a, i * 128:(i + 1) * 128], ident),
                         [("onT", a), "cst"], [("ps", b)])
                o = octr % 4
                octr += 1
                if o % 2 == 0:
                    act(ost[o], ps[b][:, :], AF.Copy, [("ps", b)], [("ost", o)])
                else:
                    cp("dve", ost[o], ps[b][:, :], [("ps", b)], [("ost", o)])
                dma("sp", y_d[i * 128:(i + 1) * 128, cg * 512:(cg + 1) * 512], ost[o], [("ost", o)], [("y", i, cg)], ("ost", o))
        S.barrier()
        S.op("sp", None, [], [])
        S.emit(nc, st)
    return nc


def _host_consts(j):
    cst = np.zeros((128, NCST), np.float32)
    cst[:, C_ID:C_ID + 128] = np.eye(128, dtype=np.float32)
    jj = np.arange(128)[:, None]
    ii = np.arange(128)[None, :]
    cst[:, C_MT:C_MT + 128] = ((jj // 64) <= (ii // 64)).astype(np.float32)
    for r in range(3):
        cst[:, C_SEL + r] = 1.0 if r == j - 1 else 0.0
    cst[:, C_SEL + 3] = 1.0 if j == 0 else 0.0
    b0 = C_B0
    cst[:, b0 + CB_AVGD:b0 + CB_AVGD + 128] = 1.0 / 2048.0
    blk = (jj // 64) == (ii // 64)
    cst[:, b0 + CB_AVG64:b0 + CB_AVG64 + 128] = blk.astype(np.float32) / 64.0
    cst[:, b0 + CB_AVG128:b0 + CB_AVG128 + 128] = 1.0 / 128.0
    cst[:, b0 + CB_NEGU:b0 + CB_NEGU + 128] = -(jj >= ii).astype(np.float32)
    cst[:, b0 + CB_NEGONE:b0 + CB_NEGONE + 128] = -1.0
    for r in range(4):
        if r < j:
            m = np.ones((128, 128), np.float32)
        elif r == j:
            m = (jj < ii).astype(np.float32)
        else:
            m = np.zeros((128, 128), np.float32)
        cst[:, b0 + CB_MASK3 + r * 128:b0 + CB_MASK3 + (r + 1) * 128] = m
    return cst


def _fm(v, nchunk):
    return np.ascontiguousarray(np.asarray(v, np.float32).reshape(nchunk, 128).T)


def _host_params(inp):
    pp = np.zeros((128, DEPTH * NPP), np.float32)
    pb = np.zeros((DEPTH, 128, 1024), np.float32)
    ws = np.zeros((128, DEPTH * 512), np.float32)
    for l in range(DEPTH):
        o = l * NPP
        pp[:, o + P_MIXG:o + P_MIXG + 16] = _fm(inp["mix_norm"][l], 16)
        cw = np.asarray(inp["conv_w"][l], np.float32)
        pp[:, o + P_CW:o + P_CW + 124] = cw.reshape(31, 4, 128).transpose(2, 1, 0).reshape(128, 124)
        pp[:, o + P_CB:o + P_CB + 4] = _fm(inp["conv_b"][l], 4)
        pp[:, o + P_CLG:o + P_CLG + 4] = _fm(inp["conv_ln_g"][l], 4)
        pp[:, o + P_CLB:o + P_CLB + 4] = _fm(inp["conv_ln_b"][l], 4)
        pp[:, o + P_MG:o + P_MG + 16] = _fm(inp["merge_norm"][l], 16)
        pp[:, o + P_FG:o + P_FG + 16] = _fm(inp["ffn_norm"][l], 16)
        fw = np.asarray(inp["ffn_conv_w"][l], np.float32)
        pp[:, o + P_FW:o + P_FW + 132] = fw.reshape(3, NFC, 128).transpose(2, 1, 0).reshape(128, 132)
        pp[:, o + P_FB:o + P_FB + NFC] = _fm(inp["ffn_conv_b"][l], NFC)
        pp[:, o + P_FIN:o + P_FIN + 16] = _fm(inp["final_norm"], 16)
        pb[l, :, 0:512] = np.asarray(inp["sg_v_norm"][l], np.float32)[None, :]
        pb[l, :, 512:1024] = np.asarray(inp["sg_b"][l], np.float32).reshape(1, 512)
        sw = np.asarray(inp["sg_w"][l], np.float32)
        ws[:, l * 512:(l + 1) * 512] = sw.transpose(2, 0, 1).reshape(128, 512)
    return pp, pb, ws


_NC_CACHE = {}
_NCORES = [8]


def kernel(**inputs):
    inp = {k: np.asarray(v) for k, v in inputs.items()}
    x = np.asarray(inp["x"], np.float32)
    pp, pb, ws = _host_params(inp)
    key = "main"
    if key not in _NC_CACHE:
        _NC_CACHE[key] = build()
    nc = _NC_CACHE[key]
    shared = {
        "w_in": np.ascontiguousarray(inp["w_in"], dtype=np.float32),
        "w_out": np.ascontiguousarray(inp["w_out"], dtype=np.float32),
        "w_gate": np.ascontiguousarray(inp["w_gate"], dtype=np.float32),
        "w_up": np.ascontiguousarray(inp["w_up"], dtype=np.float32),
        "w_down": np.ascontiguousarray(inp["w_down"], dtype=np.float32),
        "pp": pp, "pb": pb, "ws": ws,
    }
    in_maps = []
    for c in range(8):
        b, j = c // 4, c % 4
        xc = np.ascontiguousarray(x[b].reshape(32, 128, D)[j::4].reshape(T, D))
        m = dict(shared)
        m["x"] = xc
        m["cst"] = _host_consts(j)
        in_maps.append(m)
    ncores = _NCORES[0]
    in_maps = in_maps[:ncores]
    res = run_bass_kernel_spmd(nc, in_maps, core_ids=list(range(ncores)))
    out = np.zeros((2, 32, 128, D), np.float32)
    for c in range(ncores):
        b, j = c // 4, c % 4
        out[b, j::4] = np.asarray(res.results[c]["y"], np.float32).reshape(8, 128, D)
    return out.reshape(2, 4096, D)
```

```python
import numpy as np
from contextlib import ExitStack
import concourse.bass as bass
import concourse.mybir as mybir
from concourse.bass_utils import run_bass_kernel_spmd

F32 = mybir.dt.float32
BF16 = mybir.dt.bfloat16
AF = mybir.ActivationFunctionType
ALU = mybir.AluOpType
AX = mybir.AxisListType

D = 2048
T = 1024
NB = 8
DEPTH = 2
DIN = 5120
DFF = 5632
NFC = 44
EPS = 1e-6
P_MIXG, P_CW, P_CB, P_CLG, P_CLB, P_MG, P_FG, P_FW, P_FB, P_FIN, NPP = 0, 16, 140, 144, 148, 152, 168, 184, 316, 360, 376
C_ID, C_MT, C_SEL, C_B0 = 0, 128, 256, 264
CB_AVGD, CB_AVG64, CB_AVG128, CB_NEGU, CB_NEGONE, CB_ZERO, CB_MASK3, NCB = 0, 128, 256, 384, 512, 640, 768, 1280
NCST = C_B0 + NCB


class _Op:
    __slots__ = ("eng", "fn", "deps", "needs_inc", "count", "kind", "chan", "chan_idx", "sem")


class Sched:
    ENGS = ("pe", "act", "dve", "pool", "sp")

    def __init__(self):
        self.streams = {e: [] for e in self.ENGS}
        self.last_w = {}
        self.readers = {}
        self.chan_count = {}
        self.bar_set = []
        self.bar_pending = set()
        self.all_async = []
        self.last_op = {}
        self.n_cc = 0

    def op(self, eng, fn, reads=(), writes=(), kind="c", chan=None):
        o = _Op()
        o.eng, o.fn, o.kind, o.chan = eng, fn, kind, chan
        o.needs_inc = False
        o.count = 0
        o.sem = None
        deps = []
        if eng in self.bar_pending:
            deps.extend(self.bar_set)
            self.bar_pending.discard(eng)
        for k in reads:
            w = self.last_w.get(k)
            if w is not None:
                deps.append(w)
        for k in writes:
            w = self.last_w.get(k)
            if w is not None:
                deps.append(w)
            deps.extend(self.readers.get(k, ()))
        for k in reads:
            self.readers.setdefault(k, []).append(o)
        for k in writes:
            self.last_w[k] = o
            self.readers[k] = []
        seen = set()
        o.deps = []
        for d in deps:
            if id(d) in seen or d is o:
                continue
            seen.add(id(d))
            if d.kind == "c" and d.eng == "pe" and eng == "pe" and kind == "c":
                continue
            o.deps.append(d)
            if d.kind == "c":
                d.needs_inc = True
        if kind == "dma":
            assert chan is not None
            self.chan_count[chan] = self.chan_count.get(chan, 0) + 1
            o.chan_idx = self.chan_count[chan]
            self.all_async.append(o)
        elif kind == "cc":
            self.n_cc += 1
            o.chan = ("cc", self.n_cc)
            self.all_async.append(o)
        else:
            self.last_op[eng] = o
        self.streams[eng].append(o)
        return o

    def barrier(self):
        s = list(self.all_async)
        for e, o in self.last_op.items():
            s.append(o)
            o.needs_inc = True
        self.bar_set = s
        self.bar_pending = set(self.ENGS)
        self.all_async = []

    def emit(self, nc, stack):
        sems = {}
        for e in ("pe", "act", "dve", "pool"):
            sems[e] = stack.enter_context(nc.semaphore("s_" + e))
        chans = {}
        for e in self.ENGS:
            for o in self.streams[e]:
                if o.kind in ("dma", "cc") and o.chan not in chans:
                    chans[o.chan] = stack.enter_context(nc.semaphore("c%d" % len(chans)))
        for e in self.ENGS:
            cnt = 0
            for o in self.streams[e]:
                if o.kind == "c" and o.needs_inc:
                    cnt += 1
                    o.count = cnt
        block = stack.enter_context(nc.Block())

        def run(ename, eng):
            waited = {}
            for o in self.streams[ename]:
                for d in o.deps:
                    if d.kind == "dma":
                        sem, val, key = chans[d.chan], 16 * d.chan_idx, d.chan
                    elif d.kind == "cc":
                        sem, val, key = chans[d.chan], 1, d.chan
                    else:
                        sem, val, key = sems[d.eng], d.count, d.eng
                    if waited.get(key, 0) >= val:
                        continue
                    waited[key] = val
                    eng.wait_ge(sem, val)
                if o.fn is None:
                    continue
                inst = o.fn(eng)
                if o.kind == "dma":
                    inst.then_inc(chans[o.chan], 16)
                elif o.kind == "cc":
                    inst.then_inc(chans[o.chan])
                elif o.needs_inc:
                    inst.then_inc(sems[ename], 1)

        @block.tensor
        def _(eng):
            run("pe", eng)

        @block.scalar
        def _(eng):
            run("act", eng)

        @block.vector
        def _(eng):
            run("dve", eng)

        @block.gpsimd
        def _(eng):
            run("pool", eng)

        @block.sync
        def _(eng):
            run("sp", eng)


def build(dbg_stop=None, ncores=8):
    nc = bass.Bass("TRN2", target_bir_lowering=False)
    x_d = nc.dram_tensor("x", [T, D], F32, kind="ExternalInput").ap()
    w_in_d = nc.dram_tensor("w_in", [DEPTH, D, DIN], F32, kind="ExternalInput").ap()
    w_out_d = nc.dram_tensor("w_out", [DEPTH, D, D], F32, kind="ExternalInput").ap()
    w_gate_d = nc.dram_tensor("w_gate", [DEPTH, D, DFF], F32, kind="ExternalInput").ap()
    w_up_d = nc.dram_tensor("w_up", [DEPTH, D, DFF], F32, kind="ExternalInput").ap()
    w_down_d = nc.dram_tensor("w_down", [DEPTH, DFF, D], F32, kind="ExternalInput").ap()
    pp_d = nc.dram_tensor("pp", [128, DEPTH * NPP], F32, kind="ExternalInput").ap()
    pb_d = nc.dram_tensor("pb", [DEPTH, 128, 1024], F32, kind="ExternalInput").ap()
    ws_d = nc.dram_tensor("ws", [128, DEPTH * 512], F32, kind="ExternalInput").ap()
    cst_d = nc.dram_tensor("cst", [128, NCST], F32, kind="ExternalInput").ap()
    y_d = nc.dram_tensor("y", [T, D], F32, kind="ExternalOutput").ap()
    XK = [nc.dram_tensor("xk%d" % i, [128, 4096], BF16, kind="Internal").ap() for i in range(2)]
    XV = [nc.dram_tensor("xv%d" % i, [128, 4096], BF16, kind="Internal").ap() for i in range(2)]
    GK = [nc.dram_tensor("gk%d" % i, [512, 4096], BF16, kind="Internal").ap() for i in range(2)]
    GV = [nc.dram_tensor("gv%d" % i, [512, 4096], BF16, kind="Internal").ap() for i in range(2)]
    HX = nc.dram_tensor("hx", [128, 960], F32, kind="Internal").ap()
    HG = nc.dram_tensor("hg", [512, 960], F32, kind="Internal").ap()
    XHD = nc.dram_tensor("xhd", [128, 256], F32, kind="Internal").ap()
    XG = nc.dram_tensor("xg", [512, 256], F32, kind="Internal").ap()
    RG = [[0, 1, 2, 3], [4, 5, 6, 7]] if ncores == 8 else [[0, 1, 2, 3]]

    S = Sched()
    st = ExitStack()
    with st:
        def sb(name, shape, dt):
            return st.enter_context(nc.sbuf_tensor("sb_" + name, shape, dt))

        t_xT = sb("xT", [128, 16 * T], F32)
        t_hT = sb("hT", [128, 16 * T], BF16)
        t_qs = sb("qs", [128, 8 * T], BF16)
        t_ysg = sb("ysg", [128, 4 * T], BF16)
        t_hp = sb("hp", [128, 5056], F32)
        t_S = sb("S", [128, 8960], F32)
        t_pp = sb("pp", [128, DEPTH * NPP], F32)
        t_pb = sb("pb", [128, 1024], F32)
        t_ws = sb("ws", [128, DEPTH * 512], F32)
        t_wsm = sb("wsm", [128, 512], BF16)
        t_cst = sb("cst", [128, NCST], F32)
        t_cb = sb("cb", [128, NCB], BF16)
        ps = [st.enter_context(nc.psum_tensor("ps%d" % i, [128, 512], F32)) for i in range(8)]

        xT = t_xT[:, :].rearrange("p (c t) -> p c t", c=16)
        hT = t_hT[:, :].rearrange("p (c t) -> p c t", c=16)
        qs = t_qs[:, :].rearrange("p (c t) -> p c t", c=8)
        ysg = t_ysg[:, :].rearrange("p (c t) -> p c t", c=4)
        hp4 = t_hp[:, :].rearrange("p (c b t) -> p c b t", c=4, b=8)
        hp_cb = t_hp[:, :].rearrange("p (cb t) -> p cb t", t=158)

        def Sf(off, n):
            return t_S[:, off:off + n]

        def Sb(off, n):
            return t_S[:, off:off + n].bitcast(BF16)

        def Hf(off, n):
            return t_hp[:, off:off + n]

        def Hb(off, n):
            return t_hp[:, off:off + n].bitcast(BF16)

        ident = t_cst[:, C_ID:C_ID + 128]
        maskT = t_cst[:, C_MT:C_MT + 128]

        def sel(r):
            return t_cst[:, C_SEL + r:C_SEL + r + 1]

        avgD = t_cb[:, CB_AVGD:CB_AVGD + 128]
        avg64 = t_cb[:, CB_AVG64:CB_AVG64 + 128]
        avg128 = t_cb[:, CB_AVG128:CB_AVG128 + 128]
        negU = t_cb[:, CB_NEGU:CB_NEGU + 128]
        negone = t_cb[:, CB_NEGONE:CB_NEGONE + 128]
        zeros = t_cb[:, CB_ZERO:CB_ZERO + 128]
        mask3 = t_cb[:, CB_MASK3:CB_MASK3 + 512].rearrange("p (r q) -> p r q", r=4)

        def ppc(l, col, n=1):
            return t_pp[:, l * NPP + col:l * NPP + col + n]

        bank_ctr = [0]

        def nb():
            b = bank_ctr[0] % 8
            bank_ctr[0] += 1
            return b

        def mm(out, lhsT, rhs, start, stop, reads, writes, skip=False):
            if skip:
                S.op("pe", lambda e: e.matmul(out, lhsT, rhs, start=start, stop=stop, skip_group_check=True), reads, writes)
            else:
                S.op("pe", lambda e: e.matmul(out, lhsT, rhs, start=start, stop=stop), reads, writes)

        def act(out, in_, func, reads, writes, bias=None, scale=None):
            kw = {}
            if bias is not None:
                kw["bias"] = bias
            if scale is not None:
                kw["scale"] = scale
            S.op("act", lambda e: e.activation(out=out, in_=in_, func=func, **kw), reads, writes)

        def tt(eng, out, in0, in1, op, reads, writes):
            S.op(eng, lambda e: e.tensor_tensor(out=out, in0=in0, in1=in1, op=op), reads, writes)

        def ts(out, in0, s1, s2, op0, op1, reads, writes):
            if op1 is None:
                S.op("dve", lambda e: e.tensor_scalar(out=out, in0=in0, scalar1=s1, scalar2=None, op0=op0), reads, writes)
            else:
                S.op("dve", lambda e: e.tensor_scalar(out=out, in0=in0, scalar1=s1, scalar2=s2, op0=op0, op1=op1), reads, writes)

        def stt(out, in0, scalar, in1, op0, op1, reads, writes):
            S.op("dve", lambda e: e.scalar_tensor_tensor(out=out, in0=in0, scalar=scalar, in1=in1, op0=op0, op1=op1), reads, writes)

        def rs(out, in_, reads, writes):
            act(out, in_, AF.Ln, reads, writes, bias=EPS)
            act(out, out, AF.Exp, writes, writes, scale=-0.5)

        def cp(eng, out, in_, reads, writes):
            S.op(eng, lambda e: e.tensor_copy(out=out, in_=in_), reads, writes)

        def dma(q, out, in_, reads, writes, chan):
            S.op(q, lambda e: e.dma_start(out=out, in_=in_), reads, writes, kind="dma", chan=chan)

        def xk(c):
            return [("xT", c, 0), ("xT", c, 1)]

        def hk(c):
            return [("hT", c, 0), ("hT", c, 1)]

        dma("sp", t_cst[:, :], cst_d, [], ["cst"], "cst")
        dma("sp", t_pp[:, :], pp_d, [], ["pp"], "pp")
        dma("sp", t_ws[:, :], ws_d, [], ["ws"], "ws")
        cp("dve", t_cb[:, :], t_cst[:, C_B0:C_B0 + NCB], ["cst"], ["cb"])
        for i in range(NB):
            xin = Sf((i % 2) * 2048, 2048)
            dma("sp", xin, x_d[i * 128:(i + 1) * 128, :], [], [("xin", i % 2)], ("xin", i % 2))
            for cg in range(4):
                b = nb()
                for cc in range(4):
                    c = cg * 4 + cc
                    S.op("pe", lambda e, b=b, cc=cc, c=c, xin=xin: e.transpose(ps[b][:, cc * 128:(cc + 1) * 128], xin[:, c * 128:(c + 1) * 128], ident),
                         [("xin", i % 2), "cst"], [("ps", b)])
                eng = "act" if (cg % 2 == 0) else "dve"
                outv = xT[:, cg * 4:cg * 4 + 4, i * 128:(i + 1) * 128]
                inv = ps[b][:, :].rearrange("p (a t) -> p a t", a=4)
                if eng == "act":
                    act(outv, inv, AF.Copy, [("ps", b)], [("xT", cg * 4 + a, i // 4) for a in range(4)])
                else:
                    cp("dve", outv, inv, [("ps", b)], [("xT", cg * 4 + a, i // 4) for a in range(4)])

        def norm_to_hT(l, gcol):
            S.barrier()
            rstd = Sf(0, 1024)
            b0, b1 = nb(), nb()
            bb = (b0, b1)
            for c in range(16):
                xsq = Sb(1024 + (c % 2) * 512, 512)
                act(xsq, xT[:, c, :], AF.Square, xk(c), [("xsq", c % 2)])
                for h in range(2):
                    mm(ps[bb[h]][:, :], avgD, xsq[:, h * 512:(h + 1) * 512], c == 0, c == 15,
                       [("xsq", c % 2), "cb"], [("ps", bb[h])])
            for h in range(2):
                rs(rstd[:, h * 512:(h + 1) * 512], ps[bb[h]][:, :], [("ps", bb[h])], [("rstd", h)])
            return rstd

        def apply_norm(l, gcol, rstd):
            for c in range(16):
                for h in range(2):
                    stt(hT[:, c, h * 512:(h + 1) * 512], xT[:, c, h * 512:(h + 1) * 512], ppc(l, gcol + c),
                        rstd[:, h * 512:(h + 1) * 512], ALU.mult, ALU.mult,
                        [("xT", c, h), ("rstd", h), "pp"], [("hT", c, h)])

        def load_slab(wd2, col0, view, key):
            src = wd2[:, col0:col0 + 256].rearrange("(k p) n -> p k n", p=128)
            dma("pool", view, src, [], [key], key)

        def proj_fm(slab, slabkey, lc, banks, rhs_of, rkeys_of, nk=16):
            for k in range(nk):
                for h in range(2):
                    mm(ps[banks[h]][:, :], slab[:, k, lc * 128:(lc + 1) * 128], rhs_of(k, h), k == 0, k == nk - 1,
                       [slabkey] + rkeys_of(k, h), [("ps", banks[h])])

        def hT_rhs(k, h):
            return hT[:, k, h * 512:(h + 1) * 512]

        def hT_keys(k, h):
            return [("hT", k, h)]

        def head_norm_write(ypre, ykeys, avg, sqv, sqkey, rsv, rskey, gcolap, out_of, outkeys_of):
            act(sqv, ypre, AF.Square, ykeys, [sqkey])
            for h in range(2):
                b = nb()
                mm(ps[b][:, :], avg, sqv[:, h * 512:(h + 1) * 512], True, True, [sqkey, "cb"], [("ps", b)])
                rs(rsv[:, h * 512:(h + 1) * 512], ps[b][:, :], [("ps", b)], [(rskey, h)])
                stt(out_of(h), ypre[:, h * 512:(h + 1) * 512], gcolap, rsv[:, h * 512:(h + 1) * 512], ALU.mult, ALU.mult,
                    ykeys + [(rskey, h), "pp"], outkeys_of(h))

        for l in range(DEPTH):
            w_in_l = w_in_d[l]
            rstd = norm_to_hT(l, P_MIXG)
            apply_norm(l, P_MIXG, rstd)
            dma("sp", t_pb[:, :], pb_d[l], [], ["pb"], "pb")
            S.barrier()
            vn = Sb(0, 2048).rearrange("p (i f) -> p i f", i=8)
            wsl = [Sb(2048, 2048).rearrange("p (k n) -> p k n", k=16), Sb(4096, 2048).rearrange("p (k n) -> p k n", k=16)]
            tA = Sf(6144, 1024)
            tBf = Sf(7168, 512)
            tBb = Sb(7168, 512)
            stg = [Sb(7680, 128), Sb(7808, 128)]
            yp = Sf(7936, 1024)
            slab_i = [0]

            pre = {}

            def next_slab(col0):
                if col0 in pre:
                    return pre.pop(col0)
                i = slab_i[0] % 2
                slab_i[0] += 1
                load_slab(w_in_l, col0, wsl[i], ("wsl", i))
                return wsl[i], ("wsl", i)

            def preload(col0):
                pre[col0] = next_slab(col0)

            def allgather(src, dst, rkeys, wkeys):
                S.op("pool", lambda e: e.collective_compute("AllGather", ALU.bypass, replica_groups=RG, ins=[src], outs=[dst]),
                     rkeys, wkeys, kind="cc")

            for h4 in range(4):
                tt("dve", t_wsm[:, h4 * 128:(h4 + 1) * 128], t_ws[:, l * 512 + h4 * 128:l * 512 + (h4 + 1) * 128], maskT, ALU.mult,
                   ["ws", "cst"], [("wsm", h4)])
            vnb = t_pb[:, 0:512]
            bbc = t_pb[:, 512:1024]
            for s2 in range(2):
                slab, skey = next_slab(4608 + s2 * 256)
                for i in range(NB):
                    b = nb()
                    for k in range(16):
                        mm(ps[b][:, 0:256], hT[:, k, i * 128:(i + 1) * 128], slab[:, k, :], k == 0, k == 15,
                           [skey, ("hT", k, i // 4)], [("ps", b)])
                    vg = tBf[:, 0:256]
                    sqt = tBf[:, 256:512]
                    act(vg, ps[b][:, 0:256], AF.Gelu, [("ps", b)], ["tB"])
                    tt("dve", sqt, vg, vg, ALU.mult, ["tB"], ["tB"])
                    ssq = tA[:, 0:2]
                    S.op("dve", lambda e, ssq=ssq, sqt=sqt: e.reduce_sum(out=ssq, in_=sqt.rearrange("p (a f) -> p a f", a=2), axis=AX.X),
                         ["tB"], [("uT", 0)])
                    ts(tA[:, 2:4], ssq, 1.0 / 128.0, EPS, ALU.mult, ALU.add, [("uT", 0)], [("uT", 0)])
                    act(tA[:, 4:6], tA[:, 2:4], AF.Ln, [("uT", 0)], [("uT", 0)])
                    act(tA[:, 4:6], tA[:, 4:6], AF.Exp, [("uT", 0)], [("uT", 0)], scale=-0.5)
                    for hh in range(2):
                        h4 = s2 * 2 + hh
                        stt(vn[:, i, h4 * 128:(h4 + 1) * 128], vg[:, hh * 128:(hh + 1) * 128], tA[:, 4 + hh:5 + hh],
                            vnb[:, h4 * 128:(h4 + 1) * 128], ALU.mult, ALU.mult, ["tB", ("uT", 0), "pb"], [("vn", i, h4)])
            for s2 in range(2):
                slab, skey = next_slab(4096 + s2 * 256)
                for lc in range(2):
                    h4 = s2 * 2 + lc
                    bk = (nb(), nb())
                    proj_fm(slab, skey, lc, bk, hT_rhs, hT_keys)
                    for h in range(2):
                        act(tA[:, h * 512:(h + 1) * 512], ps[bk[h]][:, :], AF.Gelu, [("ps", bk[h])], [("uT", h)])
                    for h in range(2):
                        b = nb()
                        for u in range(4):
                            i = h * 4 + u
                            mm(ps[b][:, u * 128:(u + 1) * 128], vn[:, i, h4 * 128:(h4 + 1) * 128], t_wsm[:, h4 * 128:(h4 + 1) * 128],
                               True, True, [("vn", i, h4), ("wsm", h4)], [("ps", b)])
                        for u in range(4):
                            i = h * 4 + u
                            tt("dve", yp[:, i * 128:(i + 1) * 128], ps[b][:, u * 128:(u + 1) * 128], bbc[:, h4 * 128:(h4 + 1) * 128], ALU.add,
                               [("ps", b), "pb"], [("yp", i)])
                            tt("dve", yp[:, i * 128:(i + 1) * 128], yp[:, i * 128:(i + 1) * 128], tA[:, i * 128:(i + 1) * 128], ALU.mult,
                               [("yp", i), ("uT", h)], [("yp", i)])
                    ypk = [("yp", i) for i in range(8)]
                    head_norm_write(yp, ypk, avg128, tBb, "tB", tA, "uT", ppc(l, P_MG + 12 + h4),
                                    lambda h, h4=h4: ysg[:, h4, h * 512:(h + 1) * 512], lambda h, h4=h4: [("ysg", h4, h)])
            for s2 in range(2):
                slab_a, ka = next_slab(3072 + s2 * 256)
                slab_g, kg = next_slab(3584 + s2 * 256)
                for lc in range(2):
                    c = s2 * 2 + lc
                    ba = (nb(), nb())
                    proj_fm(slab_a, ka, lc, ba, hT_rhs, hT_keys)
                    bg = (nb(), nb())
                    proj_fm(slab_g, kg, lc, bg, hT_rhs, hT_keys)
                    for h in range(2):
                        act(tA[:, h * 512:(h + 1) * 512], ps[bg[h]][:, :], AF.Sigmoid, [("ps", bg[h])], [("uT", h)])
                        tt("dve", hp4[:, c, h * 4:(h + 1) * 4, 30:158], ps[ba[h]][:, :].rearrange("p (a t) -> p a t", a=4),
                           tA[:, h * 512:(h + 1) * 512].rearrange("p (a t) -> p a t", a=4), ALU.mult,
                           [("ps", ba[h]), ("uT", h)], [("hp", c)])
            for s4 in range(4):
                slab, skey = next_slab(1024 + s4 * 256)
                for lc in range(2):
                    c = s4 * 2 + lc
                    bk = (nb(), nb())
                    proj_fm(slab, skey, lc, bk, hT_rhs, hT_keys)
                    for h in range(2):
                        act(tBb[:, h * 512:(h + 1) * 512], ps[bk[h]][:, :], AF.Copy, [("ps", bk[h])], ["tB"])
                    dma("sp", XK[c // 4][:, (c % 4) * 1024:(c % 4 + 1) * 1024], tBb, ["tB"], [("XK", c)], "tBst")
            preload(2048)
            for half in range(2):
                allgather(XK[half], GK[half], [("XK", half * 4 + a) for a in range(4)], [("GK", half)])
            xv4 = [XV[a].rearrange("p (c i f) -> p c i f", c=4, i=8) for a in range(2)]
            vi = 0
            for s4 in range(4):
                slab, skey = next_slab(2048 + s4 * 256)
                for i in range(NB):
                    b = nb()
                    for k in range(16):
                        mm(ps[b][:, 0:256], hT[:, k, i * 128:(i + 1) * 128], slab[:, k, :], k == 0, k == 15,
                           [skey, ("hT", k, i // 4)], [("ps", b)])
                    sg_ = stg[vi % 2]
                    if vi % 2 == 0:
                        act(sg_, ps[b][:, 0:256], AF.Copy, [("ps", b)], [("stg", vi % 2)])
                    else:
                        cp("dve", sg_, ps[b][:, 0:256], [("ps", b)], [("stg", vi % 2)])
                    dma("sp", xv4[s4 // 2][:, (s4 % 2) * 2:(s4 % 2) * 2 + 2, i, :], sg_.rearrange("p (c f) -> p c f", c=2), [("stg", vi % 2)],
                        [("XV", s4, i)], ("stg", vi % 2))
                    vi += 1
            preload(0)
            for half in range(2):
                allgather(XV[half], GV[half], [("XV", half * 2 + a, i) for a in range(2) for i in range(8)], [("GV", half)])
            dma("sp", HX.rearrange("p (cb t) -> p cb t", t=30), hp_cb[:, :, 128:158], [("hp", c) for c in range(4)], ["HX"], "HXst")
            allgather(HX, HG, ["HX"], ["HG"])
            for s4 in range(4):
                slab, skey = next_slab(s4 * 256)
                for lc in range(2):
                    c = s4 * 2 + lc
                    bk = (nb(), nb())
                    proj_fm(slab, skey, lc, bk, hT_rhs, hT_keys)
                    for h in range(2):
                        act(qs[:, c, h * 512:(h + 1) * 512], ps[bk[h]][:, :], AF.Identity, [("ps", bk[h])], [("qs", c)], scale=0.125)
            S.barrier()
            hg = Sf(0, 3840)
            hg_r = hg.rearrange("p (r cb t) -> p r cb t", r=4, t=30)
            hg_rc = hg.rearrange("p (r c b t) -> p r c b t", r=4, c=4, t=30)
            acc = Sf(3840, 1024)
            xc = Sf(4864, 1024)
            sqb = Sb(5888, 512)
            dma("sp", hg.rearrange("p (r n) -> p r n", r=4), HG.rearrange("(r p) n -> p r n", p=128), ["HG"], ["hg"], "hg")
            hv = hp_cb[:, :, 0:30]
            hpk = [("hp", c) for c in range(4)]
            ts(hv, hg_r[:, 0], sel(0), None, ALU.mult, None, ["hg", "cst"], hpk)
            for r in (1, 2):
                stt(hv, hg_r[:, r], sel(r), hv, ALU.mult, ALU.add, ["hg", "cst"] + hpk, hpk)
            for c in range(4):
                stt(hp4[:, c, 1:8, 0:30], hg_rc[:, 3, c, 0:7, :], sel(3), hp4[:, c, 1:8, 0:30], ALU.mult, ALU.add,
                    ["hg", "cst", ("hp", c)], [("hp", c)])
            for c in range(4):
                acc3 = acc.rearrange("p (b t) -> p b t", b=8)
                acck = [("acc", 0), ("acc", 1)]
                act(acc3, hp4[:, c, :, 0:128], AF.Identity, [("hp", c), "pp"], acck,
                    bias=ppc(l, P_CB + c), scale=ppc(l, P_CW + c * 31))
                for k in range(1, 31):
                    stt(acc3, hp4[:, c, :, k:k + 128], ppc(l, P_CW + c * 31 + k), acc3, ALU.mult, ALU.add,
                        [("hp", c), "pp"] + acck, acck)
                act(sqb, acc, AF.Copy, acck, ["sqb"])
                for h in range(2):
                    b = nb()
                    mm(ps[b][:, :], avg64, sqb[:, h * 512:(h + 1) * 512], True, True, ["sqb", "cb"], [("ps", b)])
                    tt("dve", xc[:, h * 512:(h + 1) * 512], acc[:, h * 512:(h + 1) * 512], ps[b][:, :], ALU.subtract,
                       [("acc", h), ("ps", b)], [("xc", h)])
                act(sqb, xc, AF.Square, [("xc", 0), ("xc", 1)], ["sqb"])
                for h in range(2):
                    b = nb()
                    mm(ps[b][:, :], avg64, sqb[:, h * 512:(h + 1) * 512], True, True, ["sqb", "cb"], [("ps", b)])
                    rs(acc[:, h * 512:(h + 1) * 512], ps[b][:, :], [("ps", b)], [("acc", h)])
                    tt("dve", xc[:, h * 512:(h + 1) * 512], xc[:, h * 512:(h + 1) * 512], acc[:, h * 512:(h + 1) * 512], ALU.mult,
                       [("xc", h), ("acc", h)], [("xc", h)])
                act(xc, xc, AF.Silu, [("xc", 0), ("xc", 1), "pp"], [("xc", 0), ("xc", 1)],
                    bias=ppc(l, P_CLB + c), scale=ppc(l, P_CLG + c))
                head_norm_write(xc, [("xc", 0), ("xc", 1)], avg64, sqb, "sqb", acc, "acc", ppc(l, P_MG + 8 + c),
                                lambda h, c=c: hT[:, 8 + c, h * 512:(h + 1) * 512], lambda h, c=c: [("hT", 8 + c, h)])
            S.barrier()
            kvb = [Sb(0, 4096), Sb(4096, 4096)]
            sqa = Sb(8192, 512)
            gk_r = [GK[a].rearrange("(r p) n -> p r n", p=128) for a in range(2)]
            gv_r = [GV[a].rearrange("(r p) n -> p r n", p=128) for a in range(2)]
            hb0 = [0, 2304]
            e_t = [[Hf(hb0[h], 512), Hf(hb0[h] + 512, 512)] for h in range(2)]
            sp_t = [[Hb(hb0[h] + 1024, 256), Hb(hb0[h] + 1280, 256)] for h in range(2)]
            w_t = [Hb(hb0[h] + 1536, 256) for h in range(2)]
            SP_t = [[Hb(hb0[h] + 1792, 256), Hb(hb0[h] + 2048, 256)] for h in range(2)]
            ypre = t_pb[:, :]
            bA = [[nb(), nb()], [nb(), nb()]]
            bB = [nb(), nb()]
            bO = [nb(), nb()]

            def load_kv(c):
                buf = c % 2
                dma("sp", kvb[buf][:, 0:4096].rearrange("p (r t) -> p r t", r=4), gk_r[c // 4][:, :, (c % 4) * 1024:(c % 4 + 1) * 1024],
                    [("GK", c // 4)], [("kvK", buf)], ("kvK", buf))
                dma("sp", kvb[buf][:, 4096:8192].rearrange("p (r t) -> p r t", r=4), gv_r[c // 4][:, :, (c % 4) * 1024:(c % 4 + 1) * 1024],
                    [("GV", c // 4)], [("kvV", buf)], ("kvV", buf))

            steps = []
            for c in range(8):
                for g in range(2):
                    nkb = 16 * (g + 1)
                    for kb in range(nkb - 1, -1, -1):
                        for hh in range(2):
                            steps.append((c, g, kb, hh))
            NS = len(steps)
            sp_pp = {}
            pstep = {}

            def geom(c, g, kb, hh):
                r = kb % 4
                il = kb // 4
                u0 = max(0, il - 4 * g)
                edge = il >= 4 * g
                c0 = u0 * 128
                buf = c % 2
                KT = kvb[buf][:, 0:4096].rearrange("p (r t) -> p r t", r=4)
                VT = kvb[buf][:, 4096:8192].rearrange("p (r i f) -> p r i f", r=4, i=8)
                hb = 64 * hh
                kT_l = KT[hb:hb + 64, r, il * 128:(il + 1) * 128]
                q_r = qs[hb:hb + 64, c, g * 512 + c0:(g + 1) * 512]
                return r, il, edge, c0, buf, kT_l, q_r, VT

            def st_AE(c, g, kb, hh):
                r, il, edge, c0, buf, kT_l, q_r, VT = geom(c, g, kb, hh)
                a = kb % 2
                A = ps[bA[hh][a]][:, c0:512]
                mm(A, kT_l, q_r, True, True, [("kvK", buf), ("qs", c)], [("ps", bA[hh][a])])
                act(e_t[hh][a][:, c0:512], A, AF.Exp, [("ps", bA[hh][a])], [("e", hh, a)])

            def st_L(c, g, kb, hh):
                r, il, edge, c0, buf, kT_l, q_r, VT = geom(c, g, kb, hh)
                a = kb % 2
                nkb = 16 * (g + 1)
                if kb == nkb - 1:
                    for pp_ in range(2):
                        S.op("pool", lambda e, hh=hh, pp_=pp_: e.memset(SP_t[hh][pp_], 0.0), [], [("SP", hh, pp_)])
                    sp_pp[(c, g, hh)] = 0
                sp_ = sp_t[hh][a][:, c0:512]
                act(sp_, e_t[hh][a][:, c0:512], AF.Ln, [("e", hh, a)], [("sp", hh, a)], bias=1.0)
                if edge:
                    tt("pool", sp_[:, 0:128], sp_[:, 0:128], mask3[:, r, :], ALU.mult, [("sp", hh, a), "cb"], [("sp", hh, a)])
                p = sp_pp[(c, g, hh)]
                pstep[(c, g, kb, hh)] = p
                if kb > 0:
                    tt("dve", SP_t[hh][1 - p][:, c0:512], SP_t[hh][p][:, c0:512], sp_, ALU.add,
                       [("SP", hh, p), ("sp", hh, a)], [("SP", hh, 1 - p)])
                    sp_pp[(c, g, hh)] = 1 - p

            def st_B(c, g, kb, hh):
                r, il, edge, c0, buf, kT_l, q_r, VT = geom(c, g, kb, hh)
                a = kb % 2
                first = kb == 16 * (g + 1) - 1
                p = pstep[(c, g, kb, hh)]
                sp_ = sp_t[hh][a][:, c0:512]
                B = ps[bB[hh]][:, c0:512]
                mm(B, negU, sp_, True, False, ["cb", ("sp", hh, a)], [("ps", bB[hh])])
                mm(B, kT_l, q_r, False, first, [("kvK", buf), ("qs", c)], [("ps", bB[hh])])
                if not first:
                    mm(B, negone, SP_t[hh][p][:, c0:512], False, True, ["cb", ("SP", hh, p)], [("ps", bB[hh])])

            def st_W(c, g, kb, hh):
                r, il, edge, c0, buf, kT_l, q_r, VT = geom(c, g, kb, hh)
                w_ = w_t[hh][:, c0:512]
                act(w_, ps[bB[hh]][:, c0:512], AF.Exp, [("ps", bB[hh])], [("w", hh)])
                if edge:
                    tt("pool", w_[:, 0:128], w_[:, 0:128], mask3[:, r, :], ALU.mult, [("w", hh), "cb"], [("w", hh)])

            def st_O(c, g, kb, hh):
                r, il, edge, c0, buf, kT_l, q_r, VT = geom(c, g, kb, hh)
                first = kb == 16 * (g + 1) - 1
                if first:
                    mm(ps[bO[hh]][:, :], zeros, qs[:, c, 0:512], True, False, ["cb", ("qs", c)], [("ps", bO[hh])], skip=True)
                mm(ps[bO[hh]][:, c0:512], VT[:, r, il, :], w_t[hh][:, c0:512], False, kb == 0, [("kvV", buf), ("w", hh)], [("ps", bO[hh])], skip=True)
                if kb == 0:
                    hb = 64 * hh
                    act(ypre[hb:hb + 64, g * 512:(g + 1) * 512], ps[bO[hh]][hb:hb + 64, :], AF.Copy,
                        [("ps", bO[hh])], [("ypre", g, hh)])
                    if g == 1 and hh == 1:
                        if c + 2 < 8:
                            load_kv(c + 2)
                        ykeys = [("ypre", gg, h3) for gg in range(2) for h3 in range(2)]
                        act(sqa, ypre, AF.Square, ykeys, ["sqa"])
                        for h in range(2):
                            b = bO[h]
                            mm(ps[b][:, :], avg64, sqa[:, h * 512:(h + 1) * 512], True, True, ["sqa", "cb"], [("ps", b)])
                            rsm = t_cst[:, C_B0 + h * 512:C_B0 + (h + 1) * 512]
                            rs(rsm, ps[b][:, :], [("ps", b)], [("rsm", h)])
                            stt(hT[:, c, h * 512:(h + 1) * 512], ypre[:, h * 512:(h + 1) * 512], ppc(l, P_MG + c), rsm, ALU.mult, ALU.mult,
                                [("ypre", h, 0), ("ypre", h, 1), ("rsm", h), "pp"], [("hT", c, h)])

            load_kv(0)
            load_kv(1)
            for j in range(-4, NS + 1):
                if 0 <= j - 1 < NS:
                    st_O(*steps[j - 1])
                if 0 <= j + 4 < NS:
                    st_AE(*steps[j + 4])
                if 0 <= j + 2 < NS:
                    st_L(*steps[j + 2])
                if 0 <= j + 1 < NS:
                    st_B(*steps[j + 1])
                if 0 <= j < NS:
                    st_W(*steps[j])
            S.barrier()
            wsl = [Sb(0, 2048).rearrange("p (k n) -> p k n", k=16), Sb(2048, 2048).rearrange("p (k n) -> p k n", k=16)]

            def y_rhs(k, h):
                if k < 12:
                    return hT[:, k, h * 512:(h + 1) * 512]
                return ysg[:, k - 12, h * 512:(h + 1) * 512]

            def y_keys(k, h):
                if k < 12:
                    return [("hT", k, h)]
                return [("ysg", k - 12, h)]

            for s8 in range(8):
                i2 = s8 % 2
                load_slab(w_out_d[l], s8 * 256, wsl[i2], ("wsl", i2))
                for lc in range(2):
                    dc = s8 * 2 + lc
                    bk = (nb(), nb())
                    proj_fm(wsl[i2], ("wsl", i2), lc, bk, y_rhs, y_keys)
                    for h in range(2):
                        tt("dve", xT[:, dc, h * 512:(h + 1) * 512], ps[bk[h]][:, :], xT[:, dc, h * 512:(h + 1) * 512], ALU.add,
                           [("ps", bk[h]), ("xT", dc, h)], [("xT", dc, h)])
            if dbg_stop == ("mix", l):
                break
            S.barrier()
            xh_st = Hf(0, 256)
            xgt = Hf(256, 1024)
            xh = Hf(1280, 256)
            sqh = Hb(1536, 128)
            h2h = Hb(1664, 128)
            rsh = Hf(1792, 16)
            gpb = [Hf(1824, 520), Hf(2344, 520)]
            cbuf = [Hf(2864, 512), Hf(3376, 512)]
            xT_bct = t_xT[:, :].rearrange("p (c b t) -> p b c t", c=16, b=8)
            cp("dve", xh_st.rearrange("p (b c t) -> p b c t", b=8, c=16), xT_bct[:, :, :, 126:128],
               [k for c in range(16) for k in xk(c)], ["xh_st"])
            dma("sp", XHD, xh_st, ["xh_st"], ["XHD"], "XHDst")
            S.op("pool", lambda e: e.collective_compute("AllGather", ALU.bypass, replica_groups=RG, ins=[XHD], outs=[XG]),
                 ["XHD"], ["XG"], kind="cc")
            dma("sp", xgt.rearrange("p (r n) -> p r n", r=4), XG.rearrange("(r p) n -> p r n", p=128), ["XG"], ["xgt"], "xgt")
            xg_r = xgt.rearrange("p (r n) -> p r n", r=4)
            ts(xh, xg_r[:, 0], sel(0), None, ALU.mult, None, ["xgt", "cst"], ["xh"])
            for r in (1, 2):
                stt(xh, xg_r[:, r], sel(r), xh, ALU.mult, ALU.add, ["xgt", "cst", "xh"], ["xh"])
            stt(xh[:, 32:256], xg_r[:, 3, 0:224], sel(3), xh[:, 32:256], ALU.mult, ALU.add, ["xgt", "cst", "xh"], ["xh"])
            xh_bct = xh.rearrange("p (b c t) -> p b c t", b=8, c=16)
            sqh_cbt = sqh.rearrange("p (c b t) -> p c b t", c=16, b=8)
            h2h_cbt = h2h.rearrange("p (c b t) -> p c b t", c=16, b=8)
            h2h_c = h2h.rearrange("p (c n) -> p c n", c=16)
            sqh_c = sqh.rearrange("p (c n) -> p c n", c=16)
            bS = nb()
            for dc in range(16):
                act(sqh_cbt[:, dc], xh_bct[:, :, dc, :], AF.Square, ["xh"], [("sqh", dc)])
                mm(ps[bS][:, 0:16], avgD, sqh_c[:, dc, :], dc == 0, dc == 15, [("sqh", dc), "cb"], [("ps", bS)])
            rs(rsh, ps[bS][:, 0:16], [("ps", bS)], ["rsh"])
            for dc in range(16):
                stt(h2h_cbt[:, dc], xh_bct[:, :, dc, :], ppc(l, P_FG + dc), rsh.rearrange("p (b t) -> p b t", b=8), ALU.mult, ALU.mult,
                    ["xh", "rsh", "pp"], [("h2h", dc)])
            rstd = norm_to_hT(l, P_FG)
            apply_norm(l, P_FG, rstd)
            S.barrier()
            wg = [Sb(0, 2048).rearrange("p (k n) -> p k n", k=16), Sb(2048, 2048).rearrange("p (k n) -> p k n", k=16)]
            wu = [Sb(4096, 2048).rearrange("p (k n) -> p k n", k=16), Sb(6144, 2048).rearrange("p (k n) -> p k n", k=16)]
            wd = [t_qs[:, 0:4096].rearrange("p (f d) -> p f d", f=2), t_qs[:, 4096:8192].rearrange("p (f d) -> p f d", f=2),
                  t_ysg[:, 0:4096].rearrange("p (f d) -> p f d", f=2)]
            aT = [t_pb[:, :].bitcast(BF16).rearrange("p (f t) -> p f t", f=2), Hb(3888, 1024).rearrange("p (f t) -> p f t", f=2)]
            bG = (nb(), nb())
            bU = (nb(), nb())
            bH = nb()
            bD = [nb(), nb(), nb()]
            dctr = 0
            ectr = 0
            NG = 22

            def ffn_load(grp):
                b2, b3 = grp % 2, grp % 3
                load_slab(w_gate_d[l], grp * 256, wg[b2], ("wg", b2))
                load_slab(w_up_d[l], grp * 256, wu[b2], ("wu", b2))
                dma("pool", wd[b3], w_down_d[l][grp * 256:(grp + 1) * 256, :].rearrange("(f p) d -> p f d", p=128),
                    [], [("wd", b3)], ("wd", b3))

            ffn_load(0)
            for grp in range(NG + 1):
                buf = grp % 2
                if grp + 1 < NG:
                    ffn_load(grp + 1)
                for fl in range(2):
                    if grp < NG:
                        fc = grp * 2 + fl
                        for k in range(16):
                            for h in range(2):
                                mm(ps[bG[h]][:, :], wg[buf][:, k, fl * 128:(fl + 1) * 128], hT[:, k, h * 512:(h + 1) * 512], k == 0, k == 15,
                                   [("wg", buf), ("hT", k, h)], [("ps", bG[h])])
                            mm(ps[bH][:, 0:16], wg[buf][:, k, fl * 128:(fl + 1) * 128], h2h_c[:, k, :], k == 0, k == 15,
                               [("wg", buf), ("h2h", k)], [("ps", bH)])
                        for k in range(16):
                            for h in range(2):
                                mm(ps[bU[h]][:, :], wu[buf][:, k, fl * 128:(fl + 1) * 128], hT[:, k, h * 512:(h + 1) * 512], k == 0, k == 15,
                                   [("wu", buf), ("hT", k, h)], [("ps", bU[h])])
                        for h in range(2):
                            e2 = ectr % 2
                            ectr += 1
                            gp3 = gpb[e2].rearrange("p (b t) -> p b t", b=4)
                            cb3 = cbuf[e2].rearrange("p (b t) -> p b t", b=4)
                            G3 = ps[bG[h]][:, :].rearrange("p (b t) -> p b t", b=4)
                            act(gp3[:, :, 2:130], G3, AF.Copy, [("ps", bG[h])], [("gp", e2)])
                            cp("dve", gp3[:, :, 0:2], ps[bH][:, h * 8:(h + 1) * 8].rearrange("p (b t) -> p b t", b=4), [("ps", bH)], [("gp", e2)])
                            act(cb3, G3, AF.Identity, [("ps", bG[h]), "pp"], [("cbuf", e2)],
                                bias=ppc(l, P_FB + fc), scale=ppc(l, P_FW + fc * 3 + 2))
                            stt(cb3, gp3[:, :, 1:129], ppc(l, P_FW + fc * 3 + 1), cb3, ALU.mult, ALU.add, [("gp", e2), ("cbuf", e2), "pp"], [("cbuf", e2)])
                            stt(cb3, gp3[:, :, 0:128], ppc(l, P_FW + fc * 3 + 0), cb3, ALU.mult, ALU.add, [("gp", e2), ("cbuf", e2), "pp"], [("cbuf", e2)])
                            act(cbuf[e2], cbuf[e2], AF.Silu, [("cbuf", e2)], [("cbuf", e2)])
                            tt("dve", aT[buf][:, fl, h * 512:(h + 1) * 512], cbuf[e2], ps[bU[h]][:, :], ALU.mult,
                               [("cbuf", e2), ("ps", bU[h])], [("aT", buf, fl, h)])
                    if grp >= 1:
                        pg = grp - 1
                        pb2, pb3 = pg % 2, pg % 3
                        for dc in range(fl * 8, fl * 8 + 8):
                            for h in range(2):
                                b = bD[dctr % 3]
                                dctr += 1
                                for f2 in range(2):
                                    mm(ps[b][:, :], wd[pb3][:, f2, dc * 128:(dc + 1) * 128], aT[pb2][:, f2, h * 512:(h + 1) * 512], f2 == 0, f2 == 1,
                                       [("wd", pb3), ("aT", pb2, f2, h)], [("ps", b)])
                                tt("dve", xT[:, dc, h * 512:(h + 1) * 512], ps[b][:, :], xT[:, dc, h * 512:(h + 1) * 512], ALU.add,
                                   [("ps", b), ("xT", dc, h)], [("xT", dc, h)])
            if dbg_stop == ("ffn", l):
                break

        if dbg_stop is None:
            rstd = norm_to_hT(DEPTH - 1, P_FIN)
        else:
            S.barrier()
            rstd = None
        onT = Sf(2048, 4096).rearrange("p (c t) -> p c t", c=4)
        ost = [Sf(6144, 512), Sf(6656, 512), Sf(7168, 512), Sf(7680, 512)]
        octr = 0
        for cg in range(4):
            for a in range(4):
                c = cg * 4 + a
                if rstd is not None:
                    for h in range(2):
                        stt(onT[:, a, h * 512:(h + 1) * 512], xT[:, c, h * 512:(h + 1) * 512], ppc(DEPTH - 1, P_FIN + c),
                            rstd[:, h * 512:(h + 1) * 512], ALU.mult, ALU.mult, [("xT", c, h), ("rstd", h), "pp"], [("onT", a)])
                else:
                    cp("dve", onT[:, a, :], xT[:, c, :], xk(c), [("onT", a)])
            for i in range(NB):
                b = nb()
                for a in range(4):
                    S.op("pe", lambda e, b=b, a=a, i=i: e.transpose(ps[b][:, a * 128:(a + 1) * 128], onT[:, a, i * 128:(i + 1) * 128], ident),
                         [("onT", a), "cst"], [("ps", b)])
                o = octr % 4
                octr += 1
                if o % 2 == 0:
                    act(ost[o], ps[b][:, :], AF.Copy, [("ps", b)], [("ost", o)])
                else:
                    cp("dve", ost[o], ps[b][:, :], [("ps", b)], [("ost", o)])
                dma("sp", y_d[i * 128:(i + 1) * 128, cg * 512:(cg + 1) * 512], ost[o], [("ost", o)], [("y", i, cg)], ("ost", o))
        S.barrier()
        S.op("sp", None, [], [])
        S.emit(nc, st)
    return nc


def _host_consts(j):
    cst = np.zeros((128, NCST), np.float32)
    cst[:, C_ID:C_ID + 128] = np.eye(128, dtype=np.float32)
    jj = np.arange(128)[:, None]
    ii = np.arange(128)[None, :]
    cst[:, C_MT:C_MT + 128] = ((jj // 64) <= (ii // 64)).astype(np.float32)
    for r in range(3):
        cst[:, C_SEL + r] = 1.0 if r == j - 1 else 0.0
    cst[:, C_SEL + 3] = 1.0 if j == 0 else 0.0
    b0 = C_B0
    cst[:, b0 + CB_AVGD:b0 + CB_AVGD + 128] = 1.0 / 2048.0
    blk = (jj // 64) == (ii // 64)
    cst[:, b0 + CB_AVG64:b0 + CB_AVG64 + 128] = blk.astype(np.float32) / 64.0
    cst[:, b0 + CB_AVG128:b0 + CB_AVG128 + 128] = 1.0 / 128.0
    cst[:, b0 + CB_NEGU:b0 + CB_NEGU + 128] = -(jj >= ii).astype(np.float32)
    cst[:, b0 + CB_NEGONE:b0 + CB_NEGONE + 128] = -1.0
    for r in range(4):
        if r < j:
            m = np.ones((128, 128), np.float32)
        elif r == j:
            m = (jj < ii).astype(np.float32)
        else:
            m = np.zeros((128, 128), np.float32)
        cst[:, b0 + CB_MASK3 + r * 128:b0 + CB_MASK3 + (r + 1) * 128] = m
    return cst


def _fm(v, nchunk):
    return np.ascontiguousarray(np.asarray(v, np.float32).reshape(nchunk, 128).T)


def _host_params(inp):
    pp = np.zeros((128, DEPTH * NPP), np.float32)
    pb = np.zeros((DEPTH, 128, 1024), np.float32)
    ws = np.zeros((128, DEPTH * 512), np.float32)
    for l in range(DEPTH):
        o = l * NPP
        pp[:, o + P_MIXG:o + P_MIXG + 16] = _fm(inp["mix_norm"][l], 16)
        cw = np.asarray(inp["conv_w"][l], np.float32)
        pp[:, o + P_CW:o + P_CW + 124] = cw.reshape(31, 4, 128).transpose(2, 1, 0).reshape(128, 124)
        pp[:, o + P_CB:o + P_CB + 4] = _fm(inp["conv_b"][l], 4)
        pp[:, o + P_CLG:o + P_CLG + 4] = _fm(inp["conv_ln_g"][l], 4)
        pp[:, o + P_CLB:o + P_CLB + 4] = _fm(inp["conv_ln_b"][l], 4)
        pp[:, o + P_MG:o + P_MG + 16] = _fm(inp["merge_norm"][l], 16)
        pp[:, o + P_FG:o + P_FG + 16] = _fm(inp["ffn_norm"][l], 16)
        fw = np.asarray(inp["ffn_conv_w"][l], np.float32)
        pp[:, o + P_FW:o + P_FW + 132] = fw.reshape(3, NFC, 128).transpose(2, 1, 0).reshape(128, 132)
        pp[:, o + P_FB:o + P_FB + NFC] = _fm(inp["ffn_conv_b"][l], NFC)
        pp[:, o + P_FIN:o + P_FIN + 16] = _fm(inp["final_norm"], 16)
        pb[l, :, 0:512] = np.asarray(inp["sg_v_norm"][l], np.float32)[None, :]
        pb[l, :, 512:1024] = np.asarray(inp["sg_b"][l], np.float32).reshape(1, 512)
        sw = np.asarray(inp["sg_w"][l], np.float32)
        ws[:, l * 512:(l + 1) * 512] = sw.transpose(2, 0, 1).reshape(128, 512)
    return pp, pb, ws


_NC_CACHE = {}
_NCORES = [8]


def kernel(**inputs):
    inp = {k: np.asarray(v) for k, v in inputs.items()}
    x = np.asarray(inp["x"], np.float32)
    pp, pb, ws = _host_params(inp)
    key = "main"
    if key not in _NC_CACHE:
        _NC_CACHE[key] = build()
    nc = _NC_CACHE[key]
    shared = {
        "w_in": np.ascontiguousarray(inp["w_in"], dtype=np.float32),
        "w_out": np.ascontiguousarray(inp["w_out"], dtype=np.float32),
        "w_gate": np.ascontiguousarray(inp["w_gate"], dtype=np.float32),
        "w_up": np.ascontiguousarray(inp["w_up"], dtype=np.float32),
        "w_down": np.ascontiguousarray(inp["w_down"], dtype=np.float32),
        "pp": pp, "pb": pb, "ws": ws,
    }
    in_maps = []
    for c in range(8):
        b, j = c // 4, c % 4
        xc = np.ascontiguousarray(x[b].reshape(32, 128, D)[j::4].reshape(T, D))
        m = dict(shared)
        m["x"] = xc
        m["cst"] = _host_consts(j)
        in_maps.append(m)
    ncores = _NCORES[0]
    in_maps = in_maps[:ncores]
    res = run_bass_kernel_spmd(nc, in_maps, core_ids=list(range(ncores)))
    out = np.zeros((2, 32, 128, D), np.float32)
    for c in range(ncores):
        b, j = c // 4, c % 4
        out[b, j::4] = np.asarray(res.results[c]["y"], np.float32).reshape(8, 128, D)
    return out.reshape(2, 4096, D)
```

```python
import numpy as np
from contextlib import ExitStack
import concourse.bass as bass
import concourse.mybir as mybir
from concourse.bass_utils import run_bass_kernel_spmd

F32 = mybir.dt.float32
BF16 = mybir.dt.bfloat16
AF = mybir.ActivationFunctionType
ALU = mybir.AluOpType
AX = mybir.AxisListType

D = 2048
T = 1024
NB = 8
DEPTH = 2
DIN = 5120
DFF = 5632
NFC = 44
EPS = 1e-6
P_MIXG, P_CW, P_CB, P_CLG, P_CLB, P_MG, P_FG, P_FW, P_FB, P_FIN, NPP = 0, 16, 140, 144, 148, 152, 168, 184, 316, 360, 376
C_ID, C_MT, C_SEL, C_B0 = 0, 128, 256, 264
CB_AVGD, CB_AVG64, CB_AVG128, CB_NEGU, CB_NEGONE, CB_ZERO, CB_MASK3, NCB = 0, 128, 256, 384, 512, 640, 768, 1280
NCST = C_B0 + NCB


class _Op:
    __slots__ = ("eng", "fn", "deps", "needs_inc", "count", "kind", "chan", "chan_idx", "sem")


class Sched:
    ENGS = ("pe", "act", "dve", "pool", "sp")

    def __init__(self):
        self.streams = {e: [] for e in self.ENGS}
        self.last_w = {}
        self.readers = {}
        self.chan_count = {}
        self.bar_set = []
        self.bar_pending = set()
        self.all_async = []
        self.last_op = {}
        self.n_cc = 0

    def op(self, eng, fn, reads=(), writes=(), kind="c", chan=None):
        o = _Op()
        o.eng, o.fn, o.kind, o.chan = eng, fn, kind, chan
        o.needs_inc = False
        o.count = 0
        o.sem = None
        deps = []
        if eng in self.bar_pending:
            deps.extend(self.bar_set)
            self.bar_pending.discard(eng)
        for k in reads:
            w = self.last_w.get(k)
            if w is not None:
                deps.append(w)
        for k in writes:
            w = self.last_w.get(k)
            if w is not None:
                deps.append(w)
            deps.extend(self.readers.get(k, ()))
        for k in reads:
            self.readers.setdefault(k, []).append(o)
        for k in writes:
            self.last_w[k] = o
            self.readers[k] = []
        seen = set()
        o.deps = []
        for d in deps:
            if id(d) in seen or d is o:
                continue
            seen.add(id(d))
            if d.kind == "c" and d.eng == "pe" and eng == "pe" and kind == "c":
                continue
            o.deps.append(d)
            if d.kind == "c":
                d.needs_inc = True
        if kind == "dma":
            assert chan is not None
            self.chan_count[chan] = self.chan_count.get(chan, 0) + 1
            o.chan_idx = self.chan_count[chan]
            self.all_async.append(o)
        elif kind == "cc":
            self.n_cc += 1
            o.chan = ("cc", self.n_cc)
            self.all_async.append(o)
        else:
            self.last_op[eng] = o
        self.streams[eng].append(o)
        return o

    def barrier(self):
        s = list(self.all_async)
        for e, o in self.last_op.items():
            s.append(o)
            o.needs_inc = True
        self.bar_set = s
        self.bar_pending = set(self.ENGS)
        self.all_async = []

    def emit(self, nc, stack):
        sems = {}
        for e in ("pe", "act", "dve", "pool"):
            sems[e] = stack.enter_context(nc.semaphore("s_" + e))
        chans = {}
        for e in self.ENGS:
            for o in self.streams[e]:
                if o.kind in ("dma", "cc") and o.chan not in chans:
                    chans[o.chan] = stack.enter_context(nc.semaphore("c%d" % len(chans)))
        for e in self.ENGS:
            cnt = 0
            for o in self.streams[e]:
                if o.kind == "c" and o.needs_inc:
                    cnt += 1
                    o.count = cnt
        block = stack.enter_context(nc.Block())

        def run(ename, eng):
            waited = {}
            for o in self.streams[ename]:
                for d in o.deps:
                    if d.kind == "dma":
                        sem, val, key = chans[d.chan], 16 * d.chan_idx, d.chan
                    elif d.kind == "cc":
                        sem, val, key = chans[d.chan], 1, d.chan
                    else:
                        sem, val, key = sems[d.eng], d.count, d.eng
                    if waited.get(key, 0) >= val:
                        continue
                    waited[key] = val
                    eng.wait_ge(sem, val)
                if o.fn is None:
                    continue
                inst = o.fn(eng)
                if o.kind == "dma":
                    inst.then_inc(chans[o.chan], 16)
                elif o.kind == "cc":
                    inst.then_inc(chans[o.chan])
                elif o.needs_inc:
                    inst.then_inc(sems[ename], 1)

        @block.tensor
        def _(eng):
            run("pe", eng)

        @block.scalar
        def _(eng):
            run("act", eng)

        @block.vector
        def _(eng):
            run("dve", eng)

        @block.gpsimd
        def _(eng):
            run("pool", eng)

        @block.sync
        def _(eng):
            run("sp", eng)


def build(dbg_stop=None, ncores=8):
    nc = bass.Bass("TRN2", target_bir_lowering=False)
    x_d = nc.dram_tensor("x", [T, D], F32, kind="ExternalInput").ap()
    w_in_d = nc.dram_tensor("w_in", [DEPTH, D, DIN], F32, kind="ExternalInput").ap()
    w_out_d = nc.dram_tensor("w_out", [DEPTH, D, D], F32, kind="ExternalInput").ap()
    w_gate_d = nc.dram_tensor("w_gate", [DEPTH, D, DFF], F32, kind="ExternalInput").ap()
    w_up_d = nc.dram_tensor("w_up", [DEPTH, D, DFF], F32, kind="ExternalInput").ap()
    w_down_d = nc.dram_tensor("w_down", [DEPTH, DFF, D], F32, kind="ExternalInput").ap()
    pp_d = nc.dram_tensor("pp", [128, DEPTH * NPP], F32, kind="ExternalInput").ap()
    pb_d = nc.dram_tensor("pb", [DEPTH, 128, 1024], F32, kind="ExternalInput").ap()
    ws_d = nc.dram_tensor("ws", [128, DEPTH * 512], F32, kind="ExternalInput").ap()
    cst_d = nc.dram_tensor("cst", [128, NCST], F32, kind="ExternalInput").ap()
    y_d = nc.dram_tensor("y", [T, D], F32, kind="ExternalOutput").ap()
    XK = [nc.dram_tensor("xk%d" % i, [128, 4096], BF16, kind="Internal").ap() for i in range(2)]
    XV = [nc.dram_tensor("xv%d" % i, [128, 4096], BF16, kind="Internal").ap() for i in range(2)]
    GK = [nc.dram_tensor("gk%d" % i, [512, 4096], BF16, kind="Internal").ap() for i in range(2)]
    GV = [nc.dram_tensor("gv%d" % i, [512, 4096], BF16, kind="Internal").ap() for i in range(2)]
    HX = nc.dram_tensor("hx", [128, 960], F32, kind="Internal").ap()
    HG = nc.dram_tensor("hg", [512, 960], F32, kind="Internal").ap()
    XHD = nc.dram_tensor("xhd", [128, 256], F32, kind="Internal").ap()
    XG = nc.dram_tensor("xg", [512, 256], F32, kind="Internal").ap()
    RG = [[0, 1, 2, 3], [4, 5, 6, 7]] if ncores == 8 else [[0, 1, 2, 3]]

    S = Sched()
    st = ExitStack()
    with st:
        def sb(name, shape, dt):
            return st.enter_context(nc.sbuf_tensor("sb_" + name, shape, dt))

        t_xT = sb("xT", [128, 16 * T], F32)
        t_hT = sb("hT", [128, 16 * T], BF16)
        t_qs = sb("qs", [128, 8 * T], BF16)
        t_ysg = sb("ysg", [128, 4 * T], BF16)
        t_hp = sb("hp", [128, 5056], F32)
        t_S = sb("S", [128, 8960], F32)
        t_pp = sb("pp", [128, DEPTH * NPP], F32)
        t_pb = sb("pb", [128, 1024], F32)
        t_ws = sb("ws", [128, DEPTH * 512], F32)
        t_wsm = sb("wsm", [128, 512], BF16)
        t_cst = sb("cst", [128, NCST], F32)
        t_cb = sb("cb", [128, NCB], BF16)
        t_ew = sb("ew", [128, 1024], F32)
        ps = [st.enter_context(nc.psum_tensor("ps%d" % i, [128, 512], F32)) for i in range(8)]

        xT = t_xT[:, :].rearrange("p (c t) -> p c t", c=16)
        hT = t_hT[:, :].rearrange("p (c t) -> p c t", c=16)
        qs = t_qs[:, :].rearrange("p (c t) -> p c t", c=8)
        ysg = t_ysg[:, :].rearrange("p (c t) -> p c t", c=4)
        hp4 = t_hp[:, :].rearrange("p (c b t) -> p c b t", c=4, b=8)
        hp_cb = t_hp[:, :].rearrange("p (cb t) -> p cb t", t=158)

        def Sf(off, n):
            return t_S[:, off:off + n]

        def Sb(off, n):
            return t_S[:, off:off + n].bitcast(BF16)

        def Hf(off, n):
            return t_hp[:, off:off + n]

        def Hb(off, n):
            return t_hp[:, off:off + n].bitcast(BF16)

        ident = t_cst[:, C_ID:C_ID + 128]
        maskT = t_cst[:, C_MT:C_MT + 128]

        def sel(r):
            return t_cst[:, C_SEL + r:C_SEL + r + 1]

        avgD = t_cb[:, CB_AVGD:CB_AVGD + 128]
        avg64 = t_cb[:, CB_AVG64:CB_AVG64 + 128]
        avg128 = t_cb[:, CB_AVG128:CB_AVG128 + 128]
        negU = t_cb[:, CB_NEGU:CB_NEGU + 128]
        negone = t_cb[:, CB_NEGONE:CB_NEGONE + 128]
        zeros = t_cb[:, CB_ZERO:CB_ZERO + 128]
        mask3 = t_cb[:, CB_MASK3:CB_MASK3 + 512].rearrange("p (r q) -> p r q", r=4)

        def ppc(l, col, n=1):
            return t_pp[:, l * NPP + col:l * NPP + col + n]

        bank_ctr = [0]

        def nb():
            b = bank_ctr[0] % 8
            bank_ctr[0] += 1
            return b

        def mm(out, lhsT, rhs, start, stop, reads, writes, skip=False):
            if skip:
                S.op("pe", lambda e: e.matmul(out, lhsT, rhs, start=start, stop=stop, skip_group_check=True), reads, writes)
            else:
                S.op("pe", lambda e: e.matmul(out, lhsT, rhs, start=start, stop=stop), reads, writes)

        def act(out, in_, func, reads, writes, bias=None, scale=None):
            kw = {}
            if bias is not None:
                kw["bias"] = bias
            if scale is not None:
                kw["scale"] = scale
            S.op("act", lambda e: e.activation(out=out, in_=in_, func=func, **kw), reads, writes)

        def tt(eng, out, in0, in1, op, reads, writes):
            S.op(eng, lambda e: e.tensor_tensor(out=out, in0=in0, in1=in1, op=op), reads, writes)

        def ts(out, in0, s1, s2, op0, op1, reads, writes):
            if op1 is None:
                S.op("dve", lambda e: e.tensor_scalar(out=out, in0=in0, scalar1=s1, scalar2=None, op0=op0), reads, writes)
            else:
                S.op("dve", lambda e: e.tensor_scalar(out=out, in0=in0, scalar1=s1, scalar2=s2, op0=op0, op1=op1), reads, writes)

        def stt(out, in0, scalar, in1, op0, op1, reads, writes):
            S.op("dve", lambda e: e.scalar_tensor_tensor(out=out, in0=in0, scalar=scalar, in1=in1, op0=op0, op1=op1), reads, writes)

        def rs(out, in_, reads, writes):
            act(out, in_, AF.Ln, reads, writes, bias=EPS)
            act(out, out, AF.Exp, writes, writes, scale=-0.5)

        def cp(eng, out, in_, reads, writes):
            S.op(eng, lambda e: e.tensor_copy(out=out, in_=in_), reads, writes)

        def dma(q, out, in_, reads, writes, chan):
            S.op(q, lambda e: e.dma_start(out=out, in_=in_), reads, writes, kind="dma", chan=chan)

        def xk(c):
            return [("xT", c, 0), ("xT", c, 1)]

        def hk(c):
            return [("hT", c, 0), ("hT", c, 1)]

        dma("sp", t_cst[:, :], cst_d, [], ["cst"], "cst")
        dma("sp", t_pp[:, :], pp_d, [], ["pp"], "pp")
        dma("sp", t_ws[:, :], ws_d, [], ["ws"], "ws")
        cp("dve", t_cb[:, :], t_cst[:, C_B0:C_B0 + NCB], ["cst"], ["cb"])
        for i in range(NB):
            xin = Sf((i % 2) * 2048, 2048)
            dma("sp", xin, x_d[i * 128:(i + 1) * 128, :], [], [("xin", i % 2)], ("xin", i % 2))
            for cg in range(4):
                b = nb()
                for cc in range(4):
                    c = cg * 4 + cc
                    S.op("pe", lambda e, b=b, cc=cc, c=c, xin=xin: e.transpose(ps[b][:, cc * 128:(cc + 1) * 128], xin[:, c * 128:(c + 1) * 128], ident),
                         [("xin", i % 2), "cst"], [("ps", b)])
                eng = "act" if (cg % 2 == 0) else "dve"
                outv = xT[:, cg * 4:cg * 4 + 4, i * 128:(i + 1) * 128]
                inv = ps[b][:, :].rearrange("p (a t) -> p a t", a=4)
                if eng == "act":
                    act(outv, inv, AF.Copy, [("ps", b)], [("xT", cg * 4 + a, i // 4) for a in range(4)])
                else:
                    cp("dve", outv, inv, [("ps", b)], [("xT", cg * 4 + a, i // 4) for a in range(4)])

        def norm_to_hT(l, gcol):
            S.barrier()
            rstd = Sf(0, 1024)
            b0, b1 = nb(), nb()
            bb = (b0, b1)
            for c in range(16):
                xsq = Sb(1024 + (c % 2) * 512, 512)
                act(xsq, xT[:, c, :], AF.Square, xk(c), [("xsq", c % 2)])
                for h in range(2):
                    mm(ps[bb[h]][:, :], avgD, xsq[:, h * 512:(h + 1) * 512], c == 0, c == 15,
                       [("xsq", c % 2), "cb"], [("ps", bb[h])])
            for h in range(2):
                rs(rstd[:, h * 512:(h + 1) * 512], ps[bb[h]][:, :], [("ps", bb[h])], [("rstd", h)])
            return rstd

        def apply_norm(l, gcol, rstd):
            for c in range(16):
                for h in range(2):
                    stt(hT[:, c, h * 512:(h + 1) * 512], xT[:, c, h * 512:(h + 1) * 512], ppc(l, gcol + c),
                        rstd[:, h * 512:(h + 1) * 512], ALU.mult, ALU.mult,
                        [("xT", c, h), ("rstd", h), "pp"], [("hT", c, h)])

        def load_slab(wd2, col0, view, key):
            src = wd2[:, col0:col0 + 256].rearrange("(k p) n -> p k n", p=128)
            dma("pool", view, src, [], [key], key)

        def proj_fm(slab, slabkey, lc, banks, rhs_of, rkeys_of, nk=16):
            for k in range(nk):
                for h in range(2):
                    mm(ps[banks[h]][:, :], slab[:, k, lc * 128:(lc + 1) * 128], rhs_of(k, h), k == 0, k == nk - 1,
                       [slabkey] + rkeys_of(k, h), [("ps", banks[h])])

        def hT_rhs(k, h):
            return hT[:, k, h * 512:(h + 1) * 512]

        def hT_keys(k, h):
            return [("hT", k, h)]

        def head_norm_write(ypre, ykeys, avg, sqv, sqkey, rsv, rskey, gcolap, out_of, outkeys_of):
            act(sqv, ypre, AF.Square, ykeys, [sqkey])
            for h in range(2):
                b = nb()
                mm(ps[b][:, :], avg, sqv[:, h * 512:(h + 1) * 512], True, True, [sqkey, "cb"], [("ps", b)])
                rs(rsv[:, h * 512:(h + 1) * 512], ps[b][:, :], [("ps", b)], [(rskey, h)])
                stt(out_of(h), ypre[:, h * 512:(h + 1) * 512], gcolap, rsv[:, h * 512:(h + 1) * 512], ALU.mult, ALU.mult,
                    ykeys + [(rskey, h), "pp"], outkeys_of(h))

        for l in range(DEPTH):
            w_in_l = w_in_d[l]
            rstd = norm_to_hT(l, P_MIXG)
            apply_norm(l, P_MIXG, rstd)
            dma("sp", t_pb[:, :], pb_d[l], [], ["pb"], "pb")
            S.barrier()
            vn = Sb(0, 2048).rearrange("p (i f) -> p i f", i=8)
            wsl = [Sb(2048, 2048).rearrange("p (k n) -> p k n", k=16), Sb(4096, 2048).rearrange("p (k n) -> p k n", k=16)]
            tA = Sf(6144, 1024)
            tBf = Sf(7168, 512)
            tBb = Sb(7168, 512)
            stg = [Sb(7680, 128), Sb(7808, 128)]
            yp = Sf(7936, 1024)
            slab_i = [0]

            pre = {}

            def next_slab(col0):
                if col0 in pre:
                    return pre.pop(col0)
                i = slab_i[0] % 2
                slab_i[0] += 1
                load_slab(w_in_l, col0, wsl[i], ("wsl", i))
                return wsl[i], ("wsl", i)

            def preload(col0):
                pre[col0] = next_slab(col0)

            def allgather(src, dst, rkeys, wkeys):
                S.op("pool", lambda e: e.collective_compute("AllGather", ALU.bypass, replica_groups=RG, ins=[src], outs=[dst]),
                     rkeys, wkeys, kind="cc")

            for h4 in range(4):
                tt("dve", t_wsm[:, h4 * 128:(h4 + 1) * 128], t_ws[:, l * 512 + h4 * 128:l * 512 + (h4 + 1) * 128], maskT, ALU.mult,
                   ["ws", "cst"], [("wsm", h4)])
            vnb = t_pb[:, 0:512]
            bbc = t_pb[:, 512:1024]
            for s2 in range(2):
                slab, skey = next_slab(4608 + s2 * 256)
                for i in range(NB):
                    b = nb()
                    for k in range(16):
                        mm(ps[b][:, 0:256], hT[:, k, i * 128:(i + 1) * 128], slab[:, k, :], k == 0, k == 15,
                           [skey, ("hT", k, i // 4)], [("ps", b)])
                    vg = tBf[:, 0:256]
                    sqt = tBf[:, 256:512]
                    act(vg, ps[b][:, 0:256], AF.Gelu, [("ps", b)], ["tB"])
                    tt("dve", sqt, vg, vg, ALU.mult, ["tB"], ["tB"])
                    ssq = tA[:, 0:2]
                    S.op("dve", lambda e, ssq=ssq, sqt=sqt: e.reduce_sum(out=ssq, in_=sqt.rearrange("p (a f) -> p a f", a=2), axis=AX.X),
                         ["tB"], [("uT", 0)])
                    ts(tA[:, 2:4], ssq, 1.0 / 128.0, EPS, ALU.mult, ALU.add, [("uT", 0)], [("uT", 0)])
                    act(tA[:, 4:6], tA[:, 2:4], AF.Ln, [("uT", 0)], [("uT", 0)])
                    act(tA[:, 4:6], tA[:, 4:6], AF.Exp, [("uT", 0)], [("uT", 0)], scale=-0.5)
                    for hh in range(2):
                        h4 = s2 * 2 + hh
                        stt(vn[:, i, h4 * 128:(h4 + 1) * 128], vg[:, hh * 128:(hh + 1) * 128], tA[:, 4 + hh:5 + hh],
                            vnb[:, h4 * 128:(h4 + 1) * 128], ALU.mult, ALU.mult, ["tB", ("uT", 0), "pb"], [("vn", i, h4)])
            for s2 in range(2):
                slab, skey = next_slab(4096 + s2 * 256)
                for lc in range(2):
                    h4 = s2 * 2 + lc
                    bk = (nb(), nb())
                    proj_fm(slab, skey, lc, bk, hT_rhs, hT_keys)
                    for h in range(2):
                        act(tA[:, h * 512:(h + 1) * 512], ps[bk[h]][:, :], AF.Gelu, [("ps", bk[h])], [("uT", h)])
                    for h in range(2):
                        b = nb()
                        for u in range(4):
                            i = h * 4 + u
                            mm(ps[b][:, u * 128:(u + 1) * 128], vn[:, i, h4 * 128:(h4 + 1) * 128], t_wsm[:, h4 * 128:(h4 + 1) * 128],
                               True, True, [("vn", i, h4), ("wsm", h4)], [("ps", b)])
                        for u in range(4):
                            i = h * 4 + u
                            tt("dve", yp[:, i * 128:(i + 1) * 128], ps[b][:, u * 128:(u + 1) * 128], bbc[:, h4 * 128:(h4 + 1) * 128], ALU.add,
                               [("ps", b), "pb"], [("yp", i)])
                            tt("dve", yp[:, i * 128:(i + 1) * 128], yp[:, i * 128:(i + 1) * 128], tA[:, i * 128:(i + 1) * 128], ALU.mult,
                               [("yp", i), ("uT", h)], [("yp", i)])
                    ypk = [("yp", i) for i in range(8)]
                    head_norm_write(yp, ypk, avg128, tBb, "tB", tA, "uT", ppc(l, P_MG + 12 + h4),
                                    lambda h, h4=h4: ysg[:, h4, h * 512:(h + 1) * 512], lambda h, h4=h4: [("ysg", h4, h)])
            for s2 in range(2):
                slab_a, ka = next_slab(3072 + s2 * 256)
                slab_g, kg = next_slab(3584 + s2 * 256)
                for lc in range(2):
                    c = s2 * 2 + lc
                    ba = (nb(), nb())
                    proj_fm(slab_a, ka, lc, ba, hT_rhs, hT_keys)
                    bg = (nb(), nb())
                    proj_fm(slab_g, kg, lc, bg, hT_rhs, hT_keys)
                    for h in range(2):
                        act(tA[:, h * 512:(h + 1) * 512], ps[bg[h]][:, :], AF.Sigmoid, [("ps", bg[h])], [("uT", h)])
                        tt("dve", hp4[:, c, h * 4:(h + 1) * 4, 30:158], ps[ba[h]][:, :].rearrange("p (a t) -> p a t", a=4),
                           tA[:, h * 512:(h + 1) * 512].rearrange("p (a t) -> p a t", a=4), ALU.mult,
                           [("ps", ba[h]), ("uT", h)], [("hp", c)])
            for s4 in range(4):
                slab, skey = next_slab(1024 + s4 * 256)
                for lc in range(2):
                    c = s4 * 2 + lc
                    bk = (nb(), nb())
                    proj_fm(slab, skey, lc, bk, hT_rhs, hT_keys)
                    for h in range(2):
                        act(tBb[:, h * 512:(h + 1) * 512], ps[bk[h]][:, :], AF.Copy, [("ps", bk[h])], ["tB"])
                    dma("sp", XK[c // 4][:, (c % 4) * 1024:(c % 4 + 1) * 1024], tBb, ["tB"], [("XK", c)], "tBst")
            preload(2048)
            for half in range(2):
                allgather(XK[half], GK[half], [("XK", half * 4 + a) for a in range(4)], [("GK", half)])
            xv4 = [XV[a].rearrange("p (c i f) -> p c i f", c=4, i=8) for a in range(2)]
            vi = 0
            for s4 in range(4):
                slab, skey = next_slab(2048 + s4 * 256)
                for i in range(NB):
                    b = nb()
                    for k in range(16):
                        mm(ps[b][:, 0:256], hT[:, k, i * 128:(i + 1) * 128], slab[:, k, :], k == 0, k == 15,
                           [skey, ("hT", k, i // 4)], [("ps", b)])
                    sg_ = stg[vi % 2]
                    if vi % 2 == 0:
                        act(sg_, ps[b][:, 0:256], AF.Copy, [("ps", b)], [("stg", vi % 2)])
                    else:
                        cp("dve", sg_, ps[b][:, 0:256], [("ps", b)], [("stg", vi % 2)])
                    dma("sp", xv4[s4 // 2][:, (s4 % 2) * 2:(s4 % 2) * 2 + 2, i, :], sg_.rearrange("p (c f) -> p c f", c=2), [("stg", vi % 2)],
                        [("XV", s4, i)], ("stg", vi % 2))
                    vi += 1
            preload(0)
            for half in range(2):
                allgather(XV[half], GV[half], [("XV", half * 2 + a, i) for a in range(2) for i in range(8)], [("GV", half)])
            dma("sp", HX.rearrange("p (cb t) -> p cb t", t=30), hp_cb[:, :, 128:158], [("hp", c) for c in range(4)], ["HX"], "HXst")
            allgather(HX, HG, ["HX"], ["HG"])
            for s4 in range(4):
                slab, skey = next_slab(s4 * 256)
                for lc in range(2):
                    c = s4 * 2 + lc
                    bk = (nb(), nb())
                    proj_fm(slab, skey, lc, bk, hT_rhs, hT_keys)
                    for h in range(2):
                        act(qs[:, c, h * 512:(h + 1) * 512], ps[bk[h]][:, :], AF.Identity, [("ps", bk[h])], [("qs", c)], scale=0.125)
            S.barrier()
            hg = Sf(0, 3840)
            hg_r = hg.rearrange("p (r cb t) -> p r cb t", r=4, t=30)
            hg_rc = hg.rearrange("p (r c b t) -> p r c b t", r=4, c=4, t=30)
            acc = Sf(3840, 1024)
            xc = Sf(4864, 1024)
            sqb = Sb(5888, 512)
            dma("sp", hg.rearrange("p (r n) -> p r n", r=4), HG.rearrange("(r p) n -> p r n", p=128), ["HG"], ["hg"], "hg")
            hv = hp_cb[:, :, 0:30]
            hpk = [("hp", c) for c in range(4)]
            ts(hv, hg_r[:, 0], sel(0), None, ALU.mult, None, ["hg", "cst"], hpk)
            for r in (1, 2):
                stt(hv, hg_r[:, r], sel(r), hv, ALU.mult, ALU.add, ["hg", "cst"] + hpk, hpk)
            for c in range(4):
                stt(hp4[:, c, 1:8, 0:30], hg_rc[:, 3, c, 0:7, :], sel(3), hp4[:, c, 1:8, 0:30], ALU.mult, ALU.add,
                    ["hg", "cst", ("hp", c)], [("hp", c)])
            for c in range(4):
                acc3 = acc.rearrange("p (b t) -> p b t", b=8)
                acck = [("acc", 0), ("acc", 1)]
                act(acc3, hp4[:, c, :, 0:128], AF.Identity, [("hp", c), "pp"], acck,
                    bias=ppc(l, P_CB + c), scale=ppc(l, P_CW + c * 31))
                for k in range(1, 31):
                    stt(acc3, hp4[:, c, :, k:k + 128], ppc(l, P_CW + c * 31 + k), acc3, ALU.mult, ALU.add,
                        [("hp", c), "pp"] + acck, acck)
                act(sqb, acc, AF.Copy, acck, ["sqb"])
                for h in range(2):
                    b = nb()
                    mm(ps[b][:, :], avg64, sqb[:, h * 512:(h + 1) * 512], True, True, ["sqb", "cb"], [("ps", b)])
                    tt("dve", xc[:, h * 512:(h + 1) * 512], acc[:, h * 512:(h + 1) * 512], ps[b][:, :], ALU.subtract,
                       [("acc", h), ("ps", b)], [("xc", h)])
                act(sqb, xc, AF.Square, [("xc", 0), ("xc", 1)], ["sqb"])
                for h in range(2):
                    b = nb()
                    mm(ps[b][:, :], avg64, sqb[:, h * 512:(h + 1) * 512], True, True, ["sqb", "cb"], [("ps", b)])
                    rs(acc[:, h * 512:(h + 1) * 512], ps[b][:, :], [("ps", b)], [("acc", h)])
                    tt("dve", xc[:, h * 512:(h + 1) * 512], xc[:, h * 512:(h + 1) * 512], acc[:, h * 512:(h + 1) * 512], ALU.mult,
                       [("xc", h), ("acc", h)], [("xc", h)])
                act(xc, xc, AF.Silu, [("xc", 0), ("xc", 1), "pp"], [("xc", 0), ("xc", 1)],
                    bias=ppc(l, P_CLB + c), scale=ppc(l, P_CLG + c))
                head_norm_write(xc, [("xc", 0), ("xc", 1)], avg64, sqb, "sqb", acc, "acc", ppc(l, P_MG + 8 + c),
                                lambda h, c=c: hT[:, 8 + c, h * 512:(h + 1) * 512], lambda h, c=c: [("hT", 8 + c, h)])
            S.barrier()
            kvb = [Sb(0, 4096), Sb(4096, 4096)]
            sqa = Sb(8192, 512)
            gk_r = [GK[a].rearrange("(r p) n -> p r n", p=128) for a in range(2)]
            gv_r = [GV[a].rearrange("(r p) n -> p r n", p=128) for a in range(2)]
            hb0 = [0, 2304]
            e_t = [[Hf(hb0[h], 512), Hf(hb0[h] + 512, 512)] for h in range(2)]
            sp_t = [[Hb(hb0[h] + 1024, 256), Hb(hb0[h] + 1280, 256)] for h in range(2)]
            w_t = [Hb(hb0[h] + 1536, 256) for h in range(2)]
            SP_t = [[Hb(hb0[h] + 1792, 256), Hb(hb0[h] + 2048, 256)] for h in range(2)]
            ypre = t_pb[:, :]
            ew_t = [t_ew[:, 0:512], t_ew[:, 512:1024]]
            bA = [[nb(), nb()], [nb(), nb()]]
            bB = [nb(), nb()]
            bO = [nb(), nb()]

            def load_kv(c):
                buf = c % 2
                dma("sp", kvb[buf][:, 0:4096].rearrange("p (r t) -> p r t", r=4), gk_r[c // 4][:, :, (c % 4) * 1024:(c % 4 + 1) * 1024],
                    [("GK", c // 4)], [("kvK", buf)], ("kvK", buf))
                dma("sp", kvb[buf][:, 4096:8192].rearrange("p (r t) -> p r t", r=4), gv_r[c // 4][:, :, (c % 4) * 1024:(c % 4 + 1) * 1024],
                    [("GV", c // 4)], [("kvV", buf)], ("kvV", buf))

            steps = []
            for c in range(8):
                for g in range(2):
                    nkb = 16 * (g + 1)
                    for kb in range(nkb - 1, -1, -1):
                        for hh in range(2):
                            steps.append((c, g, kb, hh))
            NS = len(steps)
            sp_pp = {}
            pstep = {}

            def geom(c, g, kb, hh):
                r = kb % 4
                il = kb // 4
                u0 = max(0, il - 4 * g)
                edge = il >= 4 * g
                c0 = u0 * 128
                buf = c % 2
                KT = kvb[buf][:, 0:4096].rearrange("p (r t) -> p r t", r=4)
                VT = kvb[buf][:, 4096:8192].rearrange("p (r i f) -> p r i f", r=4, i=8)
                hb = 64 * hh
                kT_l = KT[hb:hb + 64, r, il * 128:(il + 1) * 128]
                q_r = qs[hb:hb + 64, c, g * 512 + c0:(g + 1) * 512]
                return r, il, edge, c0, buf, kT_l, q_r, VT

            def st_AE(c, g, kb, hh):
                r, il, edge, c0, buf, kT_l, q_r, VT = geom(c, g, kb, hh)
                a = kb % 2
                A = ps[bA[hh][a]][:, c0:512]
                mm(A, kT_l, q_r, True, True, [("kvK", buf), ("qs", c)], [("ps", bA[hh][a])])
                act(e_t[hh][a][:, c0:512], A, AF.Exp, [("ps", bA[hh][a])], [("e", hh, a)])

            def st_L(c, g, kb, hh):
                r, il, edge, c0, buf, kT_l, q_r, VT = geom(c, g, kb, hh)
                a = kb % 2
                nkb = 16 * (g + 1)
                if kb == nkb - 1:
                    for pp_ in range(2):
                        S.op("pool", lambda e, hh=hh, pp_=pp_: e.memset(SP_t[hh][pp_], 0.0), [], [("SP", hh, pp_)])
                    sp_pp[(c, g, hh)] = 0
                sp_ = sp_t[hh][a][:, c0:512]
                act(sp_, e_t[hh][a][:, c0:512], AF.Ln, [("e", hh, a)], [("sp", hh, a)], bias=1.0)
                if edge:
                    tt("pool", sp_[:, 0:128], sp_[:, 0:128], mask3[:, r, :], ALU.mult, [("sp", hh, a), "cb"], [("sp", hh, a)])
                p = sp_pp[(c, g, hh)]
                pstep[(c, g, kb, hh)] = p
                if kb > 0:
                    tt("dve", SP_t[hh][1 - p][:, c0:512], SP_t[hh][p][:, c0:512], sp_, ALU.add,
                       [("SP", hh, p), ("sp", hh, a)], [("SP", hh, 1 - p)])
                    sp_pp[(c, g, hh)] = 1 - p

            def st_B(c, g, kb, hh):
                r, il, edge, c0, buf, kT_l, q_r, VT = geom(c, g, kb, hh)
                a = kb % 2
                first = kb == 16 * (g + 1) - 1
                p = pstep[(c, g, kb, hh)]
                sp_ = sp_t[hh][a][:, c0:512]
                B = ps[bB[hh]][:, c0:512]
                mm(B, negU, sp_, True, first, ["cb", ("sp", hh, a)], [("ps", bB[hh])])
                if not first:
                    mm(B, negone, SP_t[hh][p][:, c0:512], False, True, ["cb", ("SP", hh, p)], [("ps", bB[hh])])

            def st_W(c, g, kb, hh):
                r, il, edge, c0, buf, kT_l, q_r, VT = geom(c, g, kb, hh)
                w_ = w_t[hh][:, c0:512]
                a = kb % 2
                ew_ = ew_t[hh][:, c0:512]
                act(ew_, ps[bB[hh]][:, c0:512], AF.Exp, [("ps", bB[hh])], [("ew", hh)])
                tt("dve", w_, e_t[hh][a][:, c0:512], ew_, ALU.mult, [("e", hh, a), ("ew", hh)], [("w", hh)])
                if edge:
                    tt("pool", w_[:, 0:128], w_[:, 0:128], mask3[:, r, :], ALU.mult, [("w", hh), "cb"], [("w", hh)])

            def st_O(c, g, kb, hh):
                r, il, edge, c0, buf, kT_l, q_r, VT = geom(c, g, kb, hh)
                first = kb == 16 * (g + 1) - 1
                if first:
                    mm(ps[bO[hh]][:, :], zeros, qs[:, c, 0:512], True, False, ["cb", ("qs", c)], [("ps", bO[hh])], skip=True)
                mm(ps[bO[hh]][:, c0:512], VT[:, r, il, :], w_t[hh][:, c0:512], False, kb == 0, [("kvV", buf), ("w", hh)], [("ps", bO[hh])], skip=True)
                if kb == 0:
                    hb = 64 * hh
                    act(ypre[hb:hb + 64, g * 512:(g + 1) * 512], ps[bO[hh]][hb:hb + 64, :], AF.Copy,
                        [("ps", bO[hh])], [("ypre", g, hh)])
                    if g == 1 and hh == 1:
                        if c + 2 < 8:
                            load_kv(c + 2)
                        ykeys = [("ypre", gg, h3) for gg in range(2) for h3 in range(2)]
                        act(sqa, ypre, AF.Square, ykeys, ["sqa"])
                        for h in range(2):
                            b = bO[h]
                            mm(ps[b][:, :], avg64, sqa[:, h * 512:(h + 1) * 512], True, True, ["sqa", "cb"], [("ps", b)])
                            rsm = t_cst[:, C_B0 + h * 512:C_B0 + (h + 1) * 512]
                            rs(rsm, ps[b][:, :], [("ps", b)], [("rsm", h)])
                            stt(hT[:, c, h * 512:(h + 1) * 512], ypre[:, h * 512:(h + 1) * 512], ppc(l, P_MG + c), rsm, ALU.mult, ALU.mult,
                                [("ypre", h, 0), ("ypre", h, 1), ("rsm", h), "pp"], [("hT", c, h)])

            load_kv(0)
            load_kv(1)
            for j in range(-4, NS + 1):
                if 0 <= j - 1 < NS:
                    st_O(*steps[j - 1])
                if 0 <= j < NS:
                    st_W(*steps[j])
                if 0 <= j + 2 < NS:
                    st_L(*steps[j + 2])
                if 0 <= j + 4 < NS:
                    st_AE(*steps[j + 4])
                if 0 <= j + 1 < NS:
                    st_B(*steps[j + 1])
            S.barrier()
            wsl = [Sb(0, 2048).rearrange("p (k n) -> p k n", k=16), Sb(2048, 2048).rearrange("p (k n) -> p k n", k=16)]

            def y_rhs(k, h):
                if k < 12:
                    return hT[:, k, h * 512:(h + 1) * 512]
                return ysg[:, k - 12, h * 512:(h + 1) * 512]

            def y_keys(k, h):
                if k < 12:
                    return [("hT", k, h)]
                return [("ysg", k - 12, h)]

            for s8 in range(8):
                i2 = s8 % 2
                load_slab(w_out_d[l], s8 * 256, wsl[i2], ("wsl", i2))
                for lc in range(2):
                    dc = s8 * 2 + lc
                    bk = (nb(), nb())
                    proj_fm(wsl[i2], ("wsl", i2), lc, bk, y_rhs, y_keys)
                    for h in range(2):
                        tt("dve", xT[:, dc, h * 512:(h + 1) * 512], ps[bk[h]][:, :], xT[:, dc, h * 512:(h + 1) * 512], ALU.add,
                           [("ps", bk[h]), ("xT", dc, h)], [("xT", dc, h)])
            if dbg_stop == ("mix", l):
                break
            S.barrier()
            xh_st = Hf(0, 256)
            xgt = Hf(256, 1024)
            xh = Hf(1280, 256)
            sqh = Hb(1536, 128)
            h2h = Hb(1664, 128)
            rsh = Hf(1792, 16)
            gpb = [Hf(1824, 520), Hf(2344, 520)]
            cbuf = [Hf(2864, 512), Hf(3376, 512)]
            xT_bct = t_xT[:, :].rearrange("p (c b t) -> p b c t", c=16, b=8)
            cp("dve", xh_st.rearrange("p (b c t) -> p b c t", b=8, c=16), xT_bct[:, :, :, 126:128],
               [k for c in range(16) for k in xk(c)], ["xh_st"])
            dma("sp", XHD, xh_st, ["xh_st"], ["XHD"], "XHDst")
            S.op("pool", lambda e: e.collective_compute("AllGather", ALU.bypass, replica_groups=RG, ins=[XHD], outs=[XG]),
                 ["XHD"], ["XG"], kind="cc")
            dma("sp", xgt.rearrange("p (r n) -> p r n", r=4), XG.rearrange("(r p) n -> p r n", p=128), ["XG"], ["xgt"], "xgt")
            xg_r = xgt.rearrange("p (r n) -> p r n", r=4)
            ts(xh, xg_r[:, 0], sel(0), None, ALU.mult, None, ["xgt", "cst"], ["xh"])
            for r in (1, 2):
                stt(xh, xg_r[:, r], sel(r), xh, ALU.mult, ALU.add, ["xgt", "cst", "xh"], ["xh"])
            stt(xh[:, 32:256], xg_r[:, 3, 0:224], sel(3), xh[:, 32:256], ALU.mult, ALU.add, ["xgt", "cst", "xh"], ["xh"])
            xh_bct = xh.rearrange("p (b c t) -> p b c t", b=8, c=16)
            sqh_cbt = sqh.rearrange("p (c b t) -> p c b t", c=16, b=8)
            h2h_cbt = h2h.rearrange("p (c b t) -> p c b t", c=16, b=8)
            h2h_c = h2h.rearrange("p (c n) -> p c n", c=16)
            sqh_c = sqh.rearrange("p (c n) -> p c n", c=16)
            bS = nb()
            for dc in range(16):
                act(sqh_cbt[:, dc], xh_bct[:, :, dc, :], AF.Square, ["xh"], [("sqh", dc)])
                mm(ps[bS][:, 0:16], avgD, sqh_c[:, dc, :], dc == 0, dc == 15, [("sqh", dc), "cb"], [("ps", bS)])
            rs(rsh, ps[bS][:, 0:16], [("ps", bS)], ["rsh"])
            for dc in range(16):
                stt(h2h_cbt[:, dc], xh_bct[:, :, dc, :], ppc(l, P_FG + dc), rsh.rearrange("p (b t) -> p b t", b=8), ALU.mult, ALU.mult,
                    ["xh", "rsh", "pp"], [("h2h", dc)])
            rstd = norm_to_hT(l, P_FG)
            apply_norm(l, P_FG, rstd)
            S.barrier()
            wg = [Sb(0, 2048).rearrange("p (k n) -> p k n", k=16), Sb(2048, 2048).rearrange("p (k n) -> p k n", k=16)]
            wu = [Sb(4096, 2048).rearrange("p (k n) -> p k n", k=16), Sb(6144, 2048).rearrange("p (k n) -> p k n", k=16)]
            wd = [t_qs[:, 0:4096].rearrange("p (f d) -> p f d", f=2), t_qs[:, 4096:8192].rearrange("p (f d) -> p f d", f=2),
                  t_ysg[:, 0:4096].rearrange("p (f d) -> p f d", f=2)]
            aT = [t_pb[:, :].bitcast(BF16).rearrange("p (f t) -> p f t", f=2), Hb(3888, 1024).rearrange("p (f t) -> p f t", f=2)]
            bG = (nb(), nb())
            bU = (nb(), nb())
            bH = nb()
            bD = [nb(), nb(), nb()]
            dctr = 0
            ectr = 0
            NG = 22

            def ffn_load(grp):
                b2, b3 = grp % 2, grp % 3
                load_slab(w_gate_d[l], grp * 256, wg[b2], ("wg", b2))
                load_slab(w_up_d[l], grp * 256, wu[b2], ("wu", b2))
                dma("pool", wd[b3], w_down_d[l][grp * 256:(grp + 1) * 256, :].rearrange("(f p) d -> p f d", p=128),
                    [], [("wd", b3)], ("wd", b3))

            ffn_load(0)
            for grp in range(NG + 1):
                buf = grp % 2
                if grp + 1 < NG:
                    ffn_load(grp + 1)
                for fl in range(2):
                    if grp < NG:
                        fc = grp * 2 + fl
                        for k in range(16):
                            for h in range(2):
                                mm(ps[bG[h]][:, :], wg[buf][:, k, fl * 128:(fl + 1) * 128], hT[:, k, h * 512:(h + 1) * 512], k == 0, k == 15,
                                   [("wg", buf), ("hT", k, h)], [("ps", bG[h])])
                            mm(ps[bH][:, 0:16], wg[buf][:, k, fl * 128:(fl + 1) * 128], h2h_c[:, k, :], k == 0, k == 15,
                               [("wg", buf), ("h2h", k)], [("ps", bH)])
                        for k in range(16):
                            for h in range(2):
                                mm(ps[bU[h]][:, :], wu[buf][:, k, fl * 128:(fl + 1) * 128], hT[:, k, h * 512:(h + 1) * 512], k == 0, k == 15,
                                   [("wu", buf), ("hT", k, h)], [("ps", bU[h])])
                        for h in range(2):
                            e2 = ectr % 2
                            ectr += 1
                            gp3 = gpb[e2].rearrange("p (b t) -> p b t", b=4)
                            cb3 = cbuf[e2].rearrange("p (b t) -> p b t", b=4)
                            G3 = ps[bG[h]][:, :].rearrange("p (b t) -> p b t", b=4)
                            act(gp3[:, :, 2:130], G3, AF.Copy, [("ps", bG[h])], [("gp", e2)])
                            cp("dve", gp3[:, :, 0:2], ps[bH][:, h * 8:(h + 1) * 8].rearrange("p (b t) -> p b t", b=4), [("ps", bH)], [("gp", e2)])
                            act(cb3, G3, AF.Identity, [("ps", bG[h]), "pp"], [("cbuf", e2)],
                                bias=ppc(l, P_FB + fc), scale=ppc(l, P_FW + fc * 3 + 2))
                            stt(cb3, gp3[:, :, 1:129], ppc(l, P_FW + fc * 3 + 1), cb3, ALU.mult, ALU.add, [("gp", e2), ("cbuf", e2), "pp"], [("cbuf", e2)])
                            stt(cb3, gp3[:, :, 0:128], ppc(l, P_FW + fc * 3 + 0), cb3, ALU.mult, ALU.add, [("gp", e2), ("cbuf", e2), "pp"], [("cbuf", e2)])
                            act(cbuf[e2], cbuf[e2], AF.Silu, [("cbuf", e2)], [("cbuf", e2)])
                            tt("dve", aT[buf][:, fl, h * 512:(h + 1) * 512], cbuf[e2], ps[bU[h]][:, :], ALU.mult,
                               [("cbuf", e2), ("ps", bU[h])], [("aT", buf, fl, h)])
                    if grp >= 1:
                        pg = grp - 1
                        pb2, pb3 = pg % 2, pg % 3
                        for dc in range(fl * 8, fl * 8 + 8):
                            for h in range(2):
                                b = bD[dctr % 3]
                                dctr += 1
                                for f2 in range(2):
                                    mm(ps[b][:, :], wd[pb3][:, f2, dc * 128:(dc + 1) * 128], aT[pb2][:, f2, h * 512:(h + 1) * 512], f2 == 0, f2 == 1,
                                       [("wd", pb3), ("aT", pb2, f2, h)], [("ps", b)])
                                tt("dve", xT[:, dc, h * 512:(h + 1) * 512], ps[b][:, :], xT[:, dc, h * 512:(h + 1) * 512], ALU.add,
                                   [("ps", b), ("xT", dc, h)], [("xT", dc, h)])
            if dbg_stop == ("ffn", l):
                break

        if dbg_stop is None:
            rstd = norm_to_hT(DEPTH - 1, P_FIN)
        else:
            S.barrier()
            rstd = None
        onT = Sf(2048, 4096).rearrange("p (c t) -> p c t", c=4)
        ost = [Sf(6144, 512), Sf(6656, 512), Sf(7168, 512), Sf(7680, 512)]
        octr = 0
        for cg in range(4):
            for a in range(4):
                c = cg * 4 + a
                if rstd is not None:
                    for h in range(2):
                        stt(onT[:, a, h * 512:(h + 1) * 512], xT[:, c, h * 512:(h + 1) * 512], ppc(DEPTH - 1, P_FIN + c),
                            rstd[:, h * 512:(h + 1) * 512], ALU.mult, ALU.mult, [("xT", c, h), ("rstd", h), "pp"], [("onT", a)])
                else:
                    cp("dve", onT[:, a, :], xT[:, c, :], xk(c), [("onT", a)])
            for i in range(NB):
                b = nb()
                for a in range(4):
                    S.op("pe", lambda e, b=b, a=a, i=i: e.transpose(ps[b][:, a * 128:(a + 1) * 128], onT[:, a, i * 128:(i + 1) * 128], ident),
                         [("onT", a), "cst"], [("ps", b)])
                o = octr % 4
                octr += 1
                if o % 2 == 0:
                    act(ost[o], ps[b][:, :], AF.Copy, [("ps", b)], [("ost", o)])
                else:
                    cp("dve", ost[o], ps[b][:, :], [("ps", b)], [("ost", o)])
                dma("sp", y_d[i * 128:(i + 1) * 128, cg * 512:(cg + 1) * 512], ost[o], [("ost", o)], [("y", i, cg)], ("ost", o))
        S.barrier()
        S.op("sp", None, [], [])
        S.emit(nc, st)
    return nc


def _host_consts(j):
    cst = np.zeros((128, NCST), np.float32)
    cst[:, C_ID:C_ID + 128] = np.eye(128, dtype=np.float32)
    jj = np.arange(128)[:, None]
    ii = np.arange(128)[None, :]
    cst[:, C_MT:C_MT + 128] = ((jj // 64) <= (ii // 64)).astype(np.float32)
    for r in range(3):
        cst[:, C_SEL + r] = 1.0 if r == j - 1 else 0.0
    cst[:, C_SEL + 3] = 1.0 if j == 0 else 0.0
    b0 = C_B0
    cst[:, b0 + CB_AVGD:b0 + CB_AVGD + 128] = 1.0 / 2048.0
    blk = (jj // 64) == (ii // 64)
    cst[:, b0 + CB_AVG64:b0 + CB_AVG64 + 128] = blk.astype(np.float32) / 64.0
    cst[:, b0 + CB_AVG128:b0 + CB_AVG128 + 128] = 1.0 / 128.0
    cst[:, b0 + CB_NEGU:b0 + CB_NEGU + 128] = -(jj >= ii).astype(np.float32)
    cst[:, b0 + CB_NEGONE:b0 + CB_NEGONE + 128] = -1.0
    for r in range(4):
        if r < j:
            m = np.ones((128, 128), np.float32)
        elif r == j:
            m = (jj < ii).astype(np.float32)
        else:
            m = np.zeros((128, 128), np.float32)
        cst[:, b0 + CB_MASK3 + r * 128:b0 + CB_MASK3 + (r + 1) * 128] = m
    return cst


def _fm(v, nchunk):
    return np.ascontiguousarray(np.asarray(v, np.float32).reshape(nchunk, 128).T)


def _host_params(inp):
    pp = np.zeros((128, DEPTH * NPP), np.float32)
    pb = np.zeros((DEPTH, 128, 1024), np.float32)
    ws = np.zeros((128, DEPTH * 512), np.float32)
    for l in range(DEPTH):
        o = l * NPP
        pp[:, o + P_MIXG:o + P_MIXG + 16] = _fm(inp["mix_norm"][l], 16)
        cw = np.asarray(inp["conv_w"][l], np.float32)
        pp[:, o + P_CW:o + P_CW + 124] = cw.reshape(31, 4, 128).transpose(2, 1, 0).reshape(128, 124)
        pp[:, o + P_CB:o + P_CB + 4] = _fm(inp["conv_b"][l], 4)
        pp[:, o + P_CLG:o + P_CLG + 4] = _fm(inp["conv_ln_g"][l], 4)
        pp[:, o + P_CLB:o + P_CLB + 4] = _fm(inp["conv_ln_b"][l], 4)
        pp[:, o + P_MG:o + P_MG + 16] = _fm(inp["merge_norm"][l], 16)
        pp[:, o + P_FG:o + P_FG + 16] = _fm(inp["ffn_norm"][l], 16)
        fw = np.asarray(inp["ffn_conv_w"][l], np.float32)
        pp[:, o + P_FW:o + P_FW + 132] = fw.reshape(3, NFC, 128).transpose(2, 1, 0).reshape(128, 132)
        pp[:, o + P_FB:o + P_FB + NFC] = _fm(inp["ffn_conv_b"][l], NFC)
        pp[:, o + P_FIN:o + P_FIN + 16] = _fm(inp["final_norm"], 16)
        pb[l, :, 0:512] = np.asarray(inp["sg_v_norm"][l], np.float32)[None, :]
        pb[l, :, 512:1024] = np.asarray(inp["sg_b"][l], np.float32).reshape(1, 512)
        sw = np.asarray(inp["sg_w"][l], np.float32)
        ws[:, l * 512:(l + 1) * 512] = sw.transpose(2, 0, 1).reshape(128, 512)
    return pp, pb, ws


_NC_CACHE = {}
_NCORES = [8]


def kernel(**inputs):
    inp = {k: np.asarray(v) for k, v in inputs.items()}
    x = np.asarray(inp["x"], np.float32)
    pp, pb, ws = _host_params(inp)
    key = "main"
    if key not in _NC_CACHE:
        _NC_CACHE[key] = build()
    nc = _NC_CACHE[key]
    shared = {
        "w_in": np.ascontiguousarray(inp["w_in"], dtype=np.float32),
        "w_out": np.ascontiguousarray(inp["w_out"], dtype=np.float32),
        "w_gate": np.ascontiguousarray(inp["w_gate"], dtype=np.float32),
        "w_up": np.ascontiguousarray(inp["w_up"], dtype=np.float32),
        "w_down": np.ascontiguousarray(inp["w_down"], dtype=np.float32),
        "pp": pp, "pb": pb, "ws": ws,
    }
    in_maps = []
    for c in range(8):
        b, j = c // 4, c % 4
        xc = np.ascontiguousarray(x[b].reshape(32, 128, D)[j::4].reshape(T, D))
        m = dict(shared)
        m["x"] = xc
        m["cst"] = _host_consts(j)
        in_maps.append(m)
    ncores = _NCORES[0]
    in_maps = in_maps[:ncores]
    res = run_bass_kernel_spmd(nc, in_maps, core_ids=list(range(ncores)))
    out = np.zeros((2, 32, 128, D), np.float32)
    for c in range(ncores):
        b, j = c // 4, c % 4
        out[b, j::4] = np.asarray(res.results[c]["y"], np.float32).reshape(8, 128, D)
    return out.reshape(2, 4096, D)
```
